# Optimizing a Trainium2 kernel written in Bass

```python
import jax, jax.numpy as jnp
from jax import lax
import numpy as np

D_MODEL = 1024
BATCH = 16
SEQ = 4096
DEPTH = 4

GLA_HEADS = 4
GLA_DK = 64
GLA_DV = 128
GLA_RANK = 16
GLA_TAU = 16.0
GLA_CHUNK = 64
SGU_GROUPS = 4
SGU_DC = 128
SGU_CHUNK = 128
D_FF = 4 * D_MODEL
EPS = 1e-6

GLA_QK = GLA_HEADS * GLA_DK
GLA_V = GLA_HEADS * GLA_DV
SGU_W = SGU_GROUPS * SGU_DC
D_MIX = GLA_V + SGU_W
IN_SIZES = (GLA_QK, GLA_QK, GLA_V, GLA_V, GLA_RANK, GLA_RANK, SGU_W, SGU_W)
D_IN = GLA_QK * 2 + GLA_V * 2 + GLA_RANK * 2 + SGU_W * 2
SPLIT_POINTS = (GLA_QK, 2 * GLA_QK, 2 * GLA_QK + GLA_V, 2 * GLA_QK + 2 * GLA_V,
                2 * GLA_QK + 2 * GLA_V + GLA_RANK, 2 * GLA_QK + 2 * GLA_V + 2 * GLA_RANK,
                2 * GLA_QK + 2 * GLA_V + 2 * GLA_RANK + SGU_W)

kernel_name = "hybrid_gla_sgu_encoder"


def rmsnorm(x, g):
    xf = x.astype(jnp.float32)
    y = xf * lax.rsqrt(jnp.mean(xf * xf, axis=-1, keepdims=True) + EPS)
    return (y * g.astype(jnp.float32)).astype(x.dtype)


def gla_one_direction(q, k, v, log_a, strict):
    B, S, H, K = q.shape
    V = v.shape[-1]
    C = GLA_CHUNK
    N = S // C
    q = q.reshape(B, N, C, H, K)
    k = k.reshape(B, N, C, H, K)
    v = v.reshape(B, N, C, H, V)
    b = jnp.cumsum(log_a.reshape(B, N, C, H, K), axis=2)
    b_mid = b[:, :, C // 2:C // 2 + 1]
    q_in = q * jnp.exp(b - b_mid)
    k_in = k * jnp.exp(b_mid - b)
    scores = jnp.einsum('bnchk,bnjhk->bnhcj', q_in, k_in)
    idx = jnp.arange(C)
    mask = (idx[:, None] > idx[None, :]) if strict else (idx[:, None] >= idx[None, :])
    scores = jnp.where(mask, scores, 0.0)
    o_intra = jnp.einsum('bnhcj,bnjhv->bnchv', scores, v)
    b_last = b[:, :, -1]
    kv = jnp.einsum('bnchk,bnchv->bnhkv', k * jnp.exp(b_last[:, :, None] - b), v)

    def step(state, inp):
        dec, kv_n = inp
        return dec[..., None] * state + kv_n, state

    init = jnp.zeros((B, H, K, V), q.dtype)
    _, s_in = lax.scan(step, init, (jnp.moveaxis(jnp.exp(b_last), 1, 0), jnp.moveaxis(kv, 1, 0)))
    s_in = jnp.moveaxis(s_in, 0, 1)
    o_inter = jnp.einsum('bnchk,bnhkv->bnchv', q * jnp.exp(b), s_in)
    return (o_intra + o_inter).reshape(B, S, H, V)


def gla_mixer(q, k, v, g, a_f, a_b, w_a2_f, b_a_f, w_a2_b, b_a_b, norm_g):
    B, S, _ = q.shape
    f32 = jnp.float32
    q = q.astype(f32).reshape(B, S, GLA_HEADS, GLA_DK) * (GLA_DK ** -0.5)
    k = k.astype(f32).reshape(B, S, GLA_HEADS, GLA_DK)
    v = v.astype(f32).reshape(B, S, GLA_HEADS, GLA_DV)
    log_a_f = (jax.nn.log_sigmoid(a_f.astype(f32) @ w_a2_f.astype(f32) + b_a_f.astype(f32)) / GLA_TAU
               ).reshape(B, S, GLA_HEADS, GLA_DK)
    log_a_b = (jax.nn.log_sigmoid(a_b.astype(f32) @ w_a2_b.astype(f32) + b_a_b.astype(f32)) / GLA_TAU
               ).reshape(B, S, GLA_HEADS, GLA_DK)
    o_f = gla_one_direction(q, k, v, log_a_f, False)
    flip = lambda t: jnp.flip(t, axis=1)
    o_b = flip(gla_one_direction(flip(q), flip(k), flip(v), flip(log_a_b), True))
    o = o_f + o_b
    o = o * lax.rsqrt(jnp.mean(o * o, axis=-1, keepdims=True) + EPS) * norm_g.astype(f32)
    return o.reshape(B, S, GLA_V) * jax.nn.silu(g.astype(f32))


def sgu_mixer(u, v, norm_g, w_s, b_s):
    B, S, _ = u.shape
    f32 = jnp.float32
    N = S // SGU_CHUNK
    v = v.astype(f32).reshape(B, N, SGU_CHUNK, SGU_GROUPS, SGU_DC)
    v = v * lax.rsqrt(jnp.mean(v * v, axis=-1, keepdims=True) + EPS) * norm_g.astype(f32)
    mixed = jnp.einsum('gpq,bnqgc->bnpgc', w_s.astype(f32), v) + b_s.astype(f32).T[:, :, None]
    return u.astype(f32) * mixed.reshape(B, S, SGU_W)


def setup_inputs(seed: int = 0) -> dict:
    key = jax.random.key(seed)
    ks = jax.random.split(key, 20)
    nrm = lambda k, shape, s: jax.random.normal(k, shape, jnp.float32) * s
    L = DEPTH
    return {
        "x": nrm(ks[0], (BATCH, SEQ, D_MODEL), 1.0),
        "norm_mix_g": 1.0 + nrm(ks[1], (L, D_MODEL), 0.02),
        "w_in": nrm(ks[2], (L, D_MODEL, D_IN), D_MODEL ** -0.5),
        "w_a2_fwd": nrm(ks[3], (L, GLA_RANK, GLA_QK), GLA_RANK ** -0.5),
        "b_a_fwd": 1.5 + nrm(ks[4], (L, GLA_QK), 1.0),
        "w_a2_bwd": nrm(ks[5], (L, GLA_RANK, GLA_QK), GLA_RANK ** -0.5),
        "b_a_bwd": 1.5 + nrm(ks[6], (L, GLA_QK), 1.0),
        "gla_norm_g": 1.0 + nrm(ks[7], (L, GLA_HEADS, GLA_DV), 0.02),
        "sgu_norm_g": 1.0 + nrm(ks[8], (L, SGU_GROUPS, SGU_DC), 0.02),
        "w_s": nrm(ks[9], (L, SGU_GROUPS, SGU_CHUNK, SGU_CHUNK), SGU_CHUNK ** -0.5),
        "b_s": 1.0 + nrm(ks[10], (L, SGU_GROUPS, SGU_CHUNK), 0.1),
        "w_out": nrm(ks[11], (L, D_MIX, D_MODEL), D_MIX ** -0.5),
        "norm_mlp_g": 1.0 + nrm(ks[12], (L, D_MODEL), 0.02),
        "w_mlp1": nrm(ks[13], (L, D_MODEL, D_FF), D_MODEL ** -0.5),
        "w_mlp2": nrm(ks[14], (L, D_FF, D_MODEL), D_FF ** -0.5),
        "final_norm_g": 1.0 + nrm(ks[15], (D_MODEL,), 0.02),
    }


def reference(x, norm_mix_g, w_in, w_a2_fwd, b_a_fwd, w_a2_bwd, b_a_bwd, gla_norm_g,
              sgu_norm_g, w_s, b_s, w_out, norm_mlp_g, w_mlp1, w_mlp2, final_norm_g):
    for l in range(DEPTH):
        h = rmsnorm(x, norm_mix_g[l])
        z = h @ w_in[l]
        q, k, v, g, a_f, a_b, su, sv = jnp.split(z, SPLIT_POINTS, axis=-1)
        o_gla = gla_mixer(q, k, v, g, a_f, a_b, w_a2_fwd[l], b_a_fwd[l],
                          w_a2_bwd[l], b_a_bwd[l], gla_norm_g[l]).astype(x.dtype)
        o_sgu = sgu_mixer(jax.nn.gelu(su, approximate=False), jax.nn.gelu(sv, approximate=False),
                          sgu_norm_g[l], w_s[l], b_s[l]).astype(x.dtype)
        x = x + jnp.concatenate([o_gla, o_sgu], axis=-1) @ w_out[l]
        h = rmsnorm(x, norm_mlp_g[l])
        x = x + jnp.square(jax.nn.relu(h @ w_mlp1[l])) @ w_mlp2[l]
    return rmsnorm(x, final_norm_g)
```

```python
import contextlib
import numpy as np
import concourse.bass as bass
import concourse.mybir as mybir
from concourse.bass_utils import run_bass_kernel_spmd

F32 = mybir.dt.float32
BF16 = mybir.dt.bfloat16
AF = mybir.ActivationFunctionType
ALU = mybir.AluOpType
AX = mybir.AxisListType

D = 1024
KC = 8
DIN = 2592
DFF = 4096
FC = 32
G = 256
TPG = 2
EPS = 1e-6
NWF = 16 * 128 + 32


class Buf:
    __slots__ = ("w", "r", "pw")

    def __init__(self):
        self.w = None
        self.r = {}
        self.pw = {}


class Prog:
    CE = ("pe", "act", "dve", "pool")

    def __init__(self, nc):
        self.nc = nc
        self.q = {e: [] for e in ("pe", "act", "dve", "pool", "sp")}
        self.n = {e: 0 for e in self.CE}
        self.waited = {e: {} for e in self.q}
        self.dcnt = {}
        self.ninst = 0

    def _wait(self, eng, key, val):
        if key == eng and eng == "pe":
            return
        if self.waited[eng].get(key, 0) >= val:
            return
        self.waited[eng][key] = val
        self.q[eng].append(("w", key, val))

    def _deps(self, eng, reads, writes, pwrites=()):
        for b in reads:
            if b.w is not None:
                self._wait(eng, *b.w)
            for k, v in b.pw.items():
                self._wait(eng, k, v)
        for b in writes:
            if b.w is not None:
                self._wait(eng, *b.w)
            for k, v in b.pw.items():
                self._wait(eng, k, v)
            for k, v in b.r.items():
                self._wait(eng, k, v)
        for b in pwrites:
            if b.w is not None:
                self._wait(eng, *b.w)
            for k, v in b.r.items():
                self._wait(eng, k, v)

    @staticmethod
    def _mark(tok, reads, writes, pwrites=()):
        k, v = tok
        for b in pwrites:
            if b.r:
                b.r = {}
                b.pw = {}
                b.w = None
            if b.pw.get(k, 0) < v:
                b.pw[k] = v
        for b in reads:
            if b.r.get(k, 0) < v:
                b.r[k] = v
        for b in writes:
            b.w = tok
            b.r = {}
            b.pw = {}

    def op(self, eng, fn, reads=(), writes=(), pwrites=()):
        self._deps(eng, reads, writes, pwrites)
        self.n[eng] += 1
        tok = (eng, self.n[eng])
        self.q[eng].append(("o", fn))
        self._mark(tok, reads, writes, pwrites)
        self.ninst += 1
        return tok

    def dma(self, out_ap, in_ap, semkey, reads=(), writes=(), eng="sp", pwrites=()):
        self._deps(eng, reads, writes, pwrites)
        c = self.dcnt.get(semkey, 0) + 16
        self.dcnt[semkey] = c
        tok = (semkey, c)
        self.q[eng].append(("d", out_ap, in_ap, semkey))
        self._mark(tok, reads, writes, pwrites)
        self.ninst += 1
        return tok

    def barrier(self):
        for e in self.q:
            for k, v in self.n.items():
                if v:
                    self._wait(e, k, v) if k != e else None
            for k, v in self.dcnt.items():
                self._wait(e, k, v)

    def flush(self, sems):
        nc = self.nc
        q = self.q

        def run(engine, items, ekey):
            for it in items:
                if it[0] == "w":
                    engine.wait_ge(sems[it[1]], it[2])
                elif it[0] == "o":
                    it[1](engine).then_inc(sems[ekey], 1)
                else:
                    engine.dma_start(out=it[1], in_=it[2]).then_inc(sems[it[3]], 16)

        with nc.Block() as block:
            @block.tensor
            def _(e):
                run(e, q["pe"], "pe")

            @block.scalar
            def _(e):
                run(e, q["act"], "act")

            @block.vector
            def _(e):
                run(e, q["dve"], "dve")

            @block.gpsimd
            def _(e):
                run(e, q["pool"], "pool")

            @block.sync
            def _(e):
                run(e, q["sp"], "sp")
        for k in q:
            q[k] = []

    def mm(self, out, lhsT, rhs, start=True, stop=True, reads=(), writes=()):
        return self.op("pe", lambda e: e.matmul(out, lhsT=lhsT, rhs=rhs, start=start, stop=stop), reads, writes)

    def act(self, out, in_, func, reads=(), writes=(), scale=1.0, bias=0.0, pwrites=()):
        return self.op("act", lambda e: e.activation(out=out, in_=in_, func=func, bias=bias, scale=scale),
                       reads, writes, pwrites)

    def tt(self, eng, out, in0, in1, op, reads=(), writes=(), pwrites=()):
        return self.op(eng, lambda e: e.tensor_tensor(out=out, in0=in0, in1=in1, op=op), reads, writes, pwrites)

    def ts(self, eng, out, in0, s1, op0, reads=(), writes=(), s2=None, op1=None, pwrites=()):
        if op1 is None:
            return self.op(eng, lambda e: e.tensor_scalar(out=out, in0=in0, scalar1=s1, scalar2=None, op0=op0),
                           reads, writes, pwrites)
        return self.op(eng, lambda e: e.tensor_scalar(out=out, in0=in0, scalar1=s1, scalar2=s2, op0=op0, op1=op1),
                       reads, writes, pwrites)

    def stt(self, eng, out, in0, scalar, in1, op0, op1, reads=(), writes=(), pwrites=()):
        return self.op(eng, lambda e: e.scalar_tensor_tensor(out=out, in0=in0, scalar=scalar, in1=in1,
                                                             op0=op0, op1=op1), reads, writes, pwrites)

    def cp(self, eng, out, in_, reads=(), writes=(), pwrites=()):
        if eng == "act":
            return self.act(out, in_, AF.Copy, reads, writes, pwrites=pwrites)
        return self.op(eng, lambda e: e.tensor_copy(out=out, in_=in_), reads, writes, pwrites)

    def memset(self, eng, ap, val, writes=()):
        return self.op(eng, lambda e: e.memset(ap, val), (), writes)


class PsumRing:
    def __init__(self, banks):
        self.banks = banks
        self.bufs = [Buf() for _ in banks]
        self.held = set()
        self.i = 0

    def next(self, hold=False):
        for _ in range(len(self.banks)):
            i = self.i
            self.i = (i + 1) % len(self.banks)
            if i not in self.held:
                if hold:
                    self.held.add(i)
                return self.banks[i], self.bufs[i]
        raise RuntimeError("all PSUM banks held")

    def release(self, buf):
        self.held.discard(self.bufs.index(buf))


def build_program(depth, nseq, seq, stop=99):
    NT = nseq * seq
    NG = seq // G
    NCH = seq // 128
    nc = bass.Bass("TRN2", target_bir_lowering=False)
    _uid = [0]

    def sbt(name, shape, dt):
        _uid[0] += 1
        return nc.sbuf_tensor(f"{name}_u{_uid[0]}", shape, dt)

    def din(name, shape):
        return nc.dram_tensor(name, list(shape), F32, kind="ExternalInput").ap()

    xT = din("xT", [D, NT])
    w_in = din("w_in", [depth * D, DIN])
    w_out = din("w_out", [depth * D, D])
    w1 = din("w1", [depth * D, DFF])
    w2 = din("w2", [depth * DFF, D])
    wa2f = din("wa2f", [depth * 16, 256])
    wa2b = din("wa2b", [depth * 16, 256])
    baf = din("baf", [depth, 256])
    bab = din("bab", [depth, 256])
    wsT = din("wsT", [depth * 4 * 128, 128])
    bsd = din("bs", [depth, 512])
    vecs_d = din("vecs", [128, 24 * depth + 8])
    cst_d = din("cst", [128, 1025])
    yT = nc.dram_tensor("yT", [D, NT], F32, kind="ExternalOutput").ap()
    xs = yT
    vscr = nc.dram_tensor("vscr", [nseq * NCH, 128, 512], BF16).ap()
    lscr = nc.dram_tensor("lscr", [nseq * NCH, 128, 512], BF16).ap()

    V_MIX = 0
    V_MLP = 8 * depth
    V_GLA = 16 * depth
    V_SGU = 20 * depth
    V_FIN = 24 * depth

    with contextlib.ExitStack() as top:
        E = top.enter_context
        sem_names = ["pe", "act", "dve", "pool", "ldx0", "ldx1", "stx0", "stx1", "ldw0", "ldw1", "ldw2", "ldc", "ldk",
                     "sl00", "sl01", "sl10", "sl11", "sv0", "sv1", "ll00", "ll01", "ll10", "ll11",
                     "lv00", "lv01", "lv10", "lv11"]
        sems = {k: E(nc.semaphore(k)) for k in sem_names}
        P = Prog(nc)
        psum = PsumRing([E(nc.psum_tensor(f"ps{i}", [128, 512], F32)) for i in range(8)])

        cst = E(sbt("cst", [128, 1025], F32))
        vecs = E(sbt("vecs", [128, 24 * depth + 8], F32))
        ugt_b = E(sbt("ugt_b", [128, 128], BF16))
        ult_b = E(sbt("ult_b", [128, 128], BF16))
        ucat_b = E(sbt("ucat_b", [128, 256], BF16))
        ones_b = E(sbt("ones_b", [128, 128], BF16))
        negc_b = E(sbt("negc_b", [128, 2], BF16))
        bconst = Buf()
        bcst = Buf()
        P.dma(cst[:], cst_d[:, :], "ldk", writes=[bcst])
        t = P.dma(vecs[:], vecs_d[:, :], "ldk", writes=[bconst])
        bcst.w = t
        maskcat = cst[:, 768:1024]
        P.cp("dve", ugt_b[:], cst[:, 0:128], reads=[bcst], writes=[bconst])
        P.cp("dve", ult_b[:], cst[:, 128:256], reads=[bcst], writes=[bconst])
        P.cp("dve", ucat_b[:], cst[:, 256:512], reads=[bcst], writes=[bconst])
        P.cp("dve", ones_b[:], cst[:, 512:640], reads=[bcst], writes=[bconst])
        P.cp("dve", negc_b[:, 0:1], cst[:, 1024:1025], reads=[bcst], writes=[bconst])

        bxs = [[Buf() for _ in range(NG)] for _ in range(nseq)]
        bvs = [[Buf() for _ in range(NCH)] for _ in range(nseq)]
        bls = [[Buf() for _ in range(NCH)] for _ in range(nseq)]

        def xview(t_ap, s, gi):
            t0 = s * seq + gi * G
            return t_ap[:, t0:t0 + G].rearrange("(kc p) t -> p kc t", p=128)

        for l in range(depth):
            last = (l == depth - 1)
            with contextlib.ExitStack() as ph:
                EP = ph.enter_context
                winf = EP(sbt("winf", [128, KC, NWF], BF16))
                wint = EP(sbt("wint", [128, KC, 1024], BF16))
                wink = EP(sbt("wink", [128, KC, 256], BF16))
                wout = EP(sbt("wout", [128, KC, D], BF16))
                waug_b = EP(sbt("waug_b", [33, 512], BF16))
                wst_b = EP(sbt("wst_b", [128, 4, 128], BF16))
                bs_bc = EP(sbt("bs_bc", [128, 512], F32))
                SS = EP(sbt("SS", [128, NCH, 512], BF16))
                dec = EP(sbt("dec", [128, NCH, 4], F32))
                bW = Buf()
                with contextlib.ExitStack() as ld:
                    EL = ld.enter_context
                    stg = [EL(sbt(f"stg{i}", [128, DIN], F32)) for i in range(2)]
                    bstg = [Buf(), Buf()]
                    waug_f = EL(sbt("waug_f", [33, 512], F32))
                    wst_f = EL(sbt("wst_f", [128, 4, 128], F32))
                    bsm = Buf()
                    P.memset("pool", waug_f[:], 0.0, writes=[bsm])
                    wv = waug_f[:].rearrange("p (h d k) -> p h d k", h=4, d=2)
                    P.dma(wv[0:16, :, 0, :], wa2f[l * 16:(l + 1) * 16, :].rearrange("p (h k) -> p h k", h=4),
                          "ldc", writes=[bsm])
                    P.dma(wv[16:32, :, 1, :], wa2b[l * 16:(l + 1) * 16, :].rearrange("p (h k) -> p h k", h=4),
                          "ldc", writes=[bsm])
                    P.dma(wv[32:33, :, 0, :], baf[l:l + 1, :].rearrange("p (h k) -> p h k", h=4), "ldc", writes=[bsm])
                    P.dma(wv[32:33, :, 1, :], bab[l:l + 1, :].rearrange("p (h k) -> p h k", h=4), "ldc", writes=[bsm])
                    P.dma(wst_f[:], wsT[l * 512:(l + 1) * 512, :].rearrange("(g q) p -> q g p", g=4), "ldc",
                          writes=[bsm])
                    t = P.dma(bs_bc[:], bsd[l:l + 1, :].partition_broadcast(128), "ldc", writes=[bsm], pwrites=[bW])
                    bsm.w = t
                    P.cp("dve", waug_b[:], waug_f[:], reads=[bsm], pwrites=[bW])
                    P.cp("pool", wst_b[:], wst_f[:], reads=[bsm], pwrites=[bW])
                    rr = 0
                    engs = ("dve", "pool")
                    for kc in range(KC):
                        sl = kc % 2
                        P.dma(stg[sl][:], w_in[l * D + kc * 128:l * D + (kc + 1) * 128, :], f"ldw{sl}",
                              writes=[bstg[sl]])
                        gm = vecs[:, V_MIX + l * 8 + kc:V_MIX + l * 8 + kc + 1]
                        s_ = stg[sl]
                        fv = winf[:, kc, 0:1024].rearrange("p (b d k) -> p b d k", b=8, d=2)
                        jobs = []
                        for d_ in range(2):
                            jobs.append((fv[:, 0:4, d_, :], s_[:, 0:256].rearrange("p (h k) -> p h k", h=4), 0.125))
                            jobs.append((fv[:, 4:8, d_, :], s_[:, 256:512].rearrange("p (h k) -> p h k", h=4), None))
                        jobs.append((wink[:, kc, :], s_[:, 256:512], None))
                        jobs.append((wint[:, kc, 0:512], s_[:, 512:1024], None))
                        jobs.append((winf[:, kc, 1024:1536], s_[:, 1024:1536], None))
                        jobs.append((winf[:, kc, 2048:2080], s_[:, 1536:1568], None))
                        jobs.append((winf[:, kc, 1536:2048], s_[:, 1568:2080], None))
                        jobs.append((wint[:, kc, 512:1024], s_[:, 2080:2592], None))
                        jeng = ("dve", "act", "dve", "pool", "act", "dve", "act", "pool", "dve", "act")
                        for ji, (o_, i_, sc) in enumerate(jobs):
                            eng = jeng[ji]
                            if sc is not None:
                                P.ts("dve", o_, i_, gm, ALU.mult, reads=[bstg[sl], bconst], pwrites=[bW],
                                     s2=sc, op1=ALU.mult)
                            elif eng == "act":
                                P.act(o_, i_, AF.Copy, reads=[bstg[sl], bconst], writes=[bW], scale=gm)
                            else:
                                P.ts(eng, o_, i_, gm, ALU.mult, reads=[bstg[sl], bconst], pwrites=[bW])
                    for kc in range(KC):
                        sl = kc % 2
                        P.dma(stg[sl][:, 0:D], w_out[l * D + kc * 128:l * D + (kc + 1) * 128, :], f"ldw{sl}",
                              writes=[bstg[sl]])
                        for hf in range(2):
                            eng = ("dve", "act")[hf]
                            o_ = wout[:, kc, hf * 512:(hf + 1) * 512]
                            i_ = stg[sl][:, hf * 512:(hf + 1) * 512]
                            if kc < 4:
                                gg = vecs[:, V_GLA + l * 4 + kc:V_GLA + l * 4 + kc + 1]
                                if eng == "act":
                                    P.act(o_, i_, AF.Copy, reads=[bstg[sl], bconst], writes=[bW], scale=gg)
                                else:
                                    P.ts(eng, o_, i_, gg, ALU.mult, reads=[bstg[sl], bconst], pwrites=[bW])
                            else:
                                P.cp(eng, o_, i_, reads=[bstg[sl]], pwrites=[bW])
                    P.barrier()
                    P.flush(sems)
                    if stop == 1:
                        return nc

                with contextlib.ExitStack() as cs:
                    EC = cs.enter_context
                    xg = [EC(sbt(f"xg{i}", [128, KC, G], F32)) for i in range(2)]
                    bxg = [[Buf() for _ in range(4)] for _ in range(2)]
                    hb = [EC(sbt(f"hb{i}", [128, KC, G], BF16)) for i in range(2)]
                    bh = [[Buf(), Buf()] for _ in range(2)]
                    sq = EC(sbt("sq", [128, KC, G], BF16)); bsq = Buf()
                    sqs = EC(sbt("sqs", [128, G], BF16)); bsqs = Buf()
                    lnt = EC(sbt("lnt", [128, G], F32)); blnt = Buf()
                    rstd = EC(sbt("rstd", [128, G], F32)); brstd = Buf()
                    aaug = [EC(sbt(f"aaug{i}", [33, G], BF16)) for i in range(2)]
                    baaug = [Buf(), Buf()]
                    esb = EC(sbt("esb", [128, 512], F32)); besb = Buf()
                    lsb = [[EC(sbt(f"lsb{i}_{t}", [128, 512], BF16)) for t in range(TPG)] for i in range(2)]
                    blsb = [[Buf() for _ in range(TPG)] for _ in range(2)]
                    edsb = [EC(sbt(f"edsb{i}", [128, 512], BF16)) for i in range(TPG)]; bedsb = [Buf(), Buf()]
                    khat = [EC(sbt(f"khat{i}", [128, 512], BF16)) for i in range(TPG)]; bkhat = [Buf(), Buf()]
                    vtok = [[EC(sbt(f"vtok{i}_{t}", [128, 512], BF16)) for t in range(TPG)] for i in range(2)]
                    bvtok = [[Buf() for _ in range(TPG)] for _ in range(2)]
                    Sst = [EC(sbt(f"Sst{i}", [128, 512], F32)) for i in range(2)]
                    Eb = EC(sbt("Eb", [128, 4, G], BF16)); bEb = Buf()
                    Einv = EC(sbt("Einv", [128, 4, G], BF16)); bEinv = Buf()
                    qt = [EC(sbt(f"qt{i}", [128, 4, G], BF16)) for i in range(2)]; bqt = [Buf(), Buf()]
                    kt = [EC(sbt(f"kt{i}", [128, 4, G], BF16)) for i in range(2)]; bkt = [Buf(), Buf()]
                    sg = [EC(sbt(f"sg{i}", [128, 4, G], BF16)) for i in range(2)]; bsg = [Buf(), Buf()]
                    gu = [EC(sbt(f"gu{i}", [128, 4, G], BF16)) for i in range(2)]; bgu = [Buf(), Buf()]
                    gsvt = [EC(sbt(f"gsv{i}", [128, 512], F32)) for i in range(TPG)]; bgsvt = [Buf(), Buf()]
                    ss4b = EC(sbt("ss4b", [128, 8], F32)); bss4b = Buf()
                    sqv = esb; bsqv = besb
                    ss4 = EC(sbt("ss4", [128, 8], F32)); bss4 = Buf()
                    vn = [[EC(sbt(f"vn{i}_{t}", [128, 4, 128], BF16)) for t in range(TPG)] for i in range(2)]
                    bvn = [[Buf() for _ in range(TPG)] for _ in range(2)]
                    PT = [EC(sbt(f"PT{i}", [128, 4, 2, 128], BF16)) for i in range(TPG)]
                    bPT = [Buf() for _ in range(TPG)]
                    osq = EC(sbt("osq", [128, 512], BF16)); bosq = Buf()
                    ro = EC(sbt("ro", [128, 512], F32)); bro = Buf()
                    om = EC(sbt("om", [128, 512], F32)); bom = Buf()
                    sgt = EC(sbt("sgt", [128, 4, 128], F32)); bsgt = Buf()
                    mT = EC(sbt("mT", [128, KC, G], BF16)); bmT = Buf()
                    xo = EC(sbt("xo", [128, 4, G], F32)); bxo = Buf()
                    bxs_h = [Buf(), Buf()]

                    for i in range(2):
                        P.memset("pool", aaug[i][32:33, :], 1.0, writes=[baaug[i]])

                    def load_x(s, gi, slot):
                        if l == 0:
                            P.dma(xg[slot][:], xview(xT, s, gi), f"ldx{slot}", writes=bxg[slot])
                        else:
                            P.dma(xg[slot][:], xview(xs, s, gi), f"ldx{slot}", reads=[bxs[s][gi]],
                                  writes=bxg[slot])

                    def front_gen(slot, mode, s_, gi_):
                        P.act(sq[:], xg[slot][:], AF.Square, reads=bxg[slot], writes=[bsq])
                        yield
                        ps, bps = psum.next()
                        for kc in range(KC):
                            P.mm(ps[:, 0:G], ones_b[:], sq[:, kc, :], start=(kc == 0), stop=(kc == KC - 1),
                                 reads=[bsq, bconst], writes=[bps])
                        P.act(lnt[:], ps[:, 0:G], AF.Ln, reads=[bps], writes=[blnt], scale=1.0 / D, bias=EPS)
                        P.act(rstd[:], lnt[:], AF.Exp, reads=[blnt], writes=[brstd], scale=-0.5)
                        for hf, eng, k0_, k1_ in ((0, "dve", 0, 6), (1, "pool", 6, 8)):
                            P.tt(eng, hb[slot][:, k0_:k1_, :], xg[slot][:, k0_:k1_, :],
                                 rstd[:].unsqueeze(1).broadcast_to([128, k1_ - k0_, G]), ALU.mult,
                                 reads=bxg[slot] + [brstd], writes=[bh[slot][hf]])
                        yield
                        if mode == "p2":
                            for t in range(TPG):
                                n_ = s_ * NCH + gi_ * TPG + t
                                P.dma(lsb[slot][t][:], lscr[n_], f"ll{slot}{t}", reads=[bls[s_][gi_ * TPG + t]],
                                      writes=[blsb[slot][t]])
                                P.dma(vtok[slot][t][:], vscr[n_], f"lv{slot}{t}", reads=[bvs[s_][gi_ * TPG + t]],
                                      writes=[bvtok[slot][t]])
                            yield
                            yield
                            return
                        pa, bpa = psum.next()
                        for kc in range(KC):
                            P.mm(pa[0:32, 0:G], winf[:, kc, 2048:2080], hb[slot][:, kc, :], start=(kc == 0),
                                 stop=(kc == KC - 1), reads=[bh[slot][kc // 6], bW], writes=[bpa])
                        P.act(aaug[slot][0:32, :], pa[0:32, 0:G], AF.Copy, reads=[bpa], writes=[baaug[slot]])
                        yield
                        for t in range(TPG):
                            pl, bpl = psum.next()
                            P.mm(pl[:, :], aaug[slot][0:33, t * 128:(t + 1) * 128], waug_b[0:33, :],
                                 reads=[baaug[slot], bW], writes=[bpl])
                            P.act(esb[:], pl[:], AF.Exp, reads=[bpl], writes=[besb], scale=-1.0)
                            P.act(lsb[slot][t][:], esb[:], AF.Ln, reads=[besb], writes=[blsb[slot][t]], bias=1.0)
                            n_ = s_ * NCH + gi_ * TPG + t
                            P.dma(lscr[n_], lsb[slot][t][:], f"sl{slot}{t}", reads=[blsb[slot][t]],
                                  writes=[bls[s_][gi_ * TPG + t]])
                        yield

                    def drive_pattern(pattern, ga, gb):
                        for ch in pattern:
                            g = ga if ch == "A" else gb
                            if g is not None:
                                next(g, None)
                        drive(ga, gb)

                    def drive(*gens):
                        gens = [g for g in gens if g is not None]
                        while gens:
                            for g in list(gens):
                                try:
                                    next(g)
                                except StopIteration:
                                    gens.remove(g)

                    for s in range(nseq):
                        bSSf = [Buf() for _ in range(NCH)]
                        bSSb = [Buf() for _ in range(NCH)]
                        bdec = [Buf() for _ in range(NCH)]
                        bS = [[[Buf() for _ in range(4)] for _ in range(2)] for _ in range(2)]
                        P.memset("dve", Sst[0][0:64, :], 0.0, writes=bS[0][0])
                        P.memset("pool", Sst[0][64:128, :], 0.0, writes=bS[1][0])

                        def fwd_step(n):
                            cur, nxt = Sst[n % 2], Sst[(n + 1) % 2]
                            for hh in range(4):
                                hs = slice(hh * 128, (hh + 1) * 128)
                                P.stt("dve", nxt[0:64, hs], cur[0:64, hs], dec[0:64, n, hh:hh + 1], SS[0:64, n, hs],
                                      ALU.mult, ALU.add, reads=[bS[0][n % 2][hh], bdec[n], bSSf[n]],
                                      writes=[bS[0][(n + 1) % 2][hh]])
                            P.cp("pool", SS[0:64, n, :], cur[0:64, :], reads=bS[0][n % 2], writes=[bSSf[n]])

                        def bwd_step(n):
                            i = NCH - 1 - n
                            cur, nxt = Sst[i % 2], Sst[(i + 1) % 2]
                            P.tt("pool", nxt[64:128, :].rearrange("p (h v) -> p h v", h=4),
                                 cur[64:128, :].rearrange("p (h v) -> p h v", h=4),
                                 dec[64:128, n, :].unsqueeze(2).broadcast_to([64, 4, 128]), ALU.mult,
                                 reads=bS[1][i % 2] + [bdec[n]], writes=bS[1][(i + 1) % 2])
                            P.tt("pool", nxt[64:128, :], nxt[64:128, :], SS[64:128, n, :], ALU.add,
                                 reads=bS[1][(i + 1) % 2] + [bSSb[n]], writes=bS[1][(i + 1) % 2])
                            P.cp("pool", SS[64:128, n, :], cur[64:128, :], reads=bS[1][i % 2], writes=[bSSb[n]])

                        def p1_tiles(gi, slot):
                            banks = []
                            for t in range(TPG):
                                tsl = slice(t * 128, (t + 1) * 128)
                                pk, bpk = psum.next(hold=True)
                                pv, bpv = psum.next(hold=True)
                                for kc in range(KC):
                                    P.mm(pk[:, 0:256], hb[slot][:, kc, tsl], wink[:, kc, :], start=(kc == 0),
                                         stop=(kc == KC - 1), reads=[bh[slot][kc // 6], bW], writes=[bpk])
                                    P.mm(pv[:, :], hb[slot][:, kc, tsl], wint[:, kc, 0:512], start=(kc == 0),
                                         stop=(kc == KC - 1), reads=[bh[slot][kc // 6], bW], writes=[bpv])
                                P.cp("act", vtok[0][t][:], pv[:], reads=[bpv], writes=[bvtok[0][t]])
                                psum.release(bpv)
                                P.dma(vscr[s * NCH + gi * TPG + t], vtok[0][t][:], f"sv{t}", reads=[bvtok[0][t]],
                                      writes=[bvs[s][gi * TPG + t]])
                                banks.append((pk, bpk, pv, bpv))
                                yield
                            for t in range(TPG):
                                n = gi * TPG + t
                                pk, bpk, pv, bpv = banks[t]
                                pd, bpd = psum.next()
                                l4 = lsb[slot][t][:].rearrange("p (h d k) -> p h d k", h=4, d=2)
                                pd4 = pd[:].rearrange("p (h d k) -> p h d k", h=4, d=2)
                                P.mm(pd4[:, :, 0, :], ugt_b[:], l4[:, :, 0, :], reads=[blsb[slot][t], bconst],
                                     writes=[bpd])
                                P.mm(pd4[:, :, 1, :], ult_b[:], l4[:, :, 1, :], reads=[blsb[slot][t], bconst],
                                     writes=[bpd])
                                pdc, bpdc = psum.next()
                                for hh in range(4):
                                    hs = slice(hh * 128, (hh + 1) * 128)
                                    P.mm(pdc[:, hh:hh + 1], lsb[slot][t][:, hs], negc_b[:, 0:1],
                                         reads=[blsb[slot][t], bconst], writes=[bpdc])
                                P.act(edsb[t][:], pd[:], AF.Exp, reads=[bpd], writes=[bedsb[t]])
                                P.act(dec[:, n, :], pdc[:, 0:4], AF.Exp, reads=[bpdc], writes=[bdec[n]])
                                k4 = pk[:, 0:256].rearrange("p (h k) -> p h k", h=4).unsqueeze(2).broadcast_to(
                                    [128, 4, 2, 64])
                                P.tt("dve", khat[t][:].rearrange("p (h d k) -> p h d k", h=4, d=2),
                                     edsb[t][:].rearrange("p (h d k) -> p h d k", h=4, d=2), k4, ALU.mult,
                                     reads=[bpk, bedsb[t]], writes=[bkhat[t]])
                                psum.release(bpk)
                                yield
                            for t in range(TPG):
                                n = gi * TPG + t
                                pkv, bpkv = psum.next()
                                for hh in range(4):
                                    hs = slice(hh * 128, (hh + 1) * 128)
                                    P.mm(pkv[:, hs], khat[t][:, hs], vtok[0][t][:, hs],
                                         reads=[bkhat[t], bvtok[0][t]], writes=[bpkv])
                                P.cp("dve", SS[:, n, :], pkv[:], reads=[bpkv], writes=[bSSf[n], bSSb[n]])
                                fwd_step(n)
                                yield

                        load_x(s, 0, 0)
                        if NG > 1:
                            load_x(s, 1, 1)
                        fgs = {g: front_gen(g % 2, "p1", s, g) for g in range(NG)}
                        drive(fgs[0])
                        if NG > 1:
                            next(fgs[1])
                        for gi in range(NG):
                            slot = gi % 2
                            if gi + 2 < NG:
                                load_x(s, gi + 2, slot)
                            fa = fgs.get(gi + 1)
                            fb = fgs.get(gi + 2)
                            tg = p1_tiles(gi, slot)
                            for ch in "ABBBBACBBA":
                                g = {"A": fa, "B": tg, "C": fb}[ch]
                                if g is not None:
                                    next(g, None)
                            drive(tg, fa)

                        def p2_A(gi, slot, fg):
                            def tok(t):
                                tsl = slice(t * 128, (t + 1) * 128)
                                pw, bpw = psum.next()
                                for kc in range(KC):
                                    P.mm(pw[:, :], hb[slot][:, kc, tsl], wint[:, kc, 512:1024], start=(kc == 0),
                                         stop=(kc == KC - 1), reads=[bh[slot][kc // 6], bW], writes=[bpw])
                                P.act(gsvt[t][:], pw[:], AF.Gelu, reads=[bpw], writes=[bgsvt[t]])
                                P.tt("pool", sqv[:], gsvt[t][:], gsvt[t][:], ALU.mult, reads=[bgsvt[t]], writes=[bsqv])
                                P.op("dve", lambda e: e.tensor_reduce(
                                    out=ss4[:, 4 * t:4 * t + 4], in_=sqv[:].rearrange("p (g c) -> p g c", g=4),
                                    axis=AX.X, op=ALU.add), reads=[bsqv], pwrites=[bss4])

                            def vnorm():
                                P.act(ss4b[:], ss4[:], AF.Ln, reads=[bss4], writes=[bss4b], scale=1.0 / 128, bias=EPS)
                                P.act(ss4b[:], ss4b[:], AF.Exp, reads=[bss4b], writes=[bss4b], scale=-0.5)
                                for t in range(TPG):
                                    P.tt("dve", vn[slot][t][:], gsvt[t][:].rearrange("p (g c) -> p g c", g=4),
                                         ss4b[:, 4 * t:4 * t + 4].unsqueeze(2).broadcast_to([128, 4, 128]), ALU.mult,
                                         reads=[bgsvt[t], bss4b], writes=[bvn[slot][t]])

                            def decays():
                                for t in range(TPG):
                                    tsl = slice(t * 128, (t + 1) * 128)
                                    for bi in range(2):
                                        pb, bpb = psum.next()
                                        for j in range(2):
                                            hh = 2 * bi + j
                                            P.mm(pb[:, j * 256:(j + 1) * 256],
                                                 lsb[slot][t][:, hh * 128:(hh + 1) * 128], ucat_b[:, :],
                                                 reads=[blsb[slot][t], bconst], writes=[bpb])
                                        pbv = pb[:].rearrange("p (h c) -> p h c", h=2)
                                        P.act(Eb[0:64, 2 * bi:2 * bi + 2, tsl], pbv[0:64, :, 0:128], AF.Exp,
                                              reads=[bpb], pwrites=[bEb])
                                        P.act(Eb[64:128, 2 * bi:2 * bi + 2, tsl], pbv[64:128, :, 128:256], AF.Exp,
                                              reads=[bpb], pwrites=[bEb])
                                        P.act(Einv[0:64, 2 * bi:2 * bi + 2, tsl], pbv[0:64, :, 0:128], AF.Exp,
                                              reads=[bpb], pwrites=[bEinv], scale=-1.0)
                                        P.act(Einv[64:128, 2 * bi:2 * bi + 2, tsl], pbv[64:128, :, 128:256], AF.Exp,
                                              reads=[bpb], pwrites=[bEinv], scale=-1.0)

                            def feat(pr):
                                pf, bpf = psum.next()
                                for j in range(2):
                                    blk = 2 * pr + j
                                    for kc in range(KC):
                                        P.mm(pf[:, j * G:(j + 1) * G], winf[:, kc, blk * 128:(blk + 1) * 128],
                                             hb[slot][:, kc, :], start=(kc == 0), stop=(kc == KC - 1),
                                             reads=[bh[slot][kc // 6], bW], writes=[bpf])
                                pfv = pf[:].rearrange("p (b t) -> p b t", b=2)
                                b2 = slice(2 * (pr % 2), 2 * (pr % 2) + 2)
                                if pr < 2:
                                    P.tt("dve", qt[slot][:, b2, :], pfv, Eb[:, b2, :], ALU.mult, reads=[bpf, bEb],
                                         pwrites=[bqt[slot]])
                                elif pr < 4:
                                    P.tt("dve", kt[slot][:, b2, :], pfv, Einv[:, b2, :], ALU.mult,
                                         reads=[bpf, bEinv], pwrites=[bkt[slot]])
                                elif pr < 6:
                                    P.act(sg[slot][:, b2, :], pfv, AF.Silu, reads=[bpf], pwrites=[bsg[slot]])
                                else:
                                    P.act(gu[slot][:, b2, :], pfv, AF.Gelu, reads=[bpf], pwrites=[bgu[slot]])

                            tok(0)
                            yield
                            tok(1)
                            yield
                            decays()
                            yield
                            for pr in (4, 5, 6, 7, 0, 1, 2, 3):
                                feat(pr)
                                yield
                            vnorm()
                            yield

                        def p2_B(gi, slot, g_next2=None):
                            po_l, pm_l = [], []
                            for t in range(TPG):
                                n = gi * TPG + t
                                tsl = slice(t * 128, (t + 1) * 128)
                                psf, bpsf = psum.next()
                                psb, bpsb = psum.next()
                                for hh in range(4):
                                    hs = slice(hh * 128, (hh + 1) * 128)
                                    P.mm(psf[:, hs], kt[slot][0:64, hh, tsl], qt[slot][0:64, hh, tsl],
                                         reads=[bkt[slot], bqt[slot]], writes=[bpsf])
                                    P.mm(psb[:, hs], kt[slot][64:128, hh, tsl], qt[slot][64:128, hh, tsl],
                                         reads=[bkt[slot], bqt[slot]], writes=[bpsb])
                                P.tt("dve", PT[t][:, :, 0, :], psf[:].rearrange("p (h c) -> p h c", h=4),
                                     cst[:, 768:896].unsqueeze(1).broadcast_to([128, 4, 128]), ALU.mult,
                                     reads=[bpsf, bcst], pwrites=[bPT[t]])
                                P.tt("dve", PT[t][:, :, 1, :], psb[:].rearrange("p (h c) -> p h c", h=4),
                                     cst[:, 896:1024].unsqueeze(1).broadcast_to([128, 4, 128]), ALU.mult,
                                     reads=[bpsb, bcst], pwrites=[bPT[t]])
                                pm, bpm = psum.next()
                                for g_ in range(4):
                                    gs = slice(g_ * 128, (g_ + 1) * 128)
                                    P.mm(pm[:, gs], vn[slot][t][:, g_, :], wst_b[:, g_, :], reads=[bvn[slot][t], bW],
                                         writes=[bpm])
                                for g_ in range(4):
                                    gs = slice(g_ * 128, (g_ + 1) * 128)
                                    P.stt("dve", sgt[:, g_, :], pm[:, gs],
                                          vecs[:, V_SGU + l * 4 + g_:V_SGU + l * 4 + g_ + 1], bs_bc[:, gs],
                                          ALU.mult, ALU.add, reads=[bpm, bconst, bW], pwrites=[bsgt])
                                P.tt("pool", mT[:, 4:8, tsl], sgt[:], gu[slot][:, :, tsl], ALU.mult,
                                     reads=[bsgt, bgu[slot]], pwrites=[bmT])
                                yield
                            for t in range(TPG):
                                n = gi * TPG + t
                                tsl = slice(t * 128, (t + 1) * 128)
                                po, bpo = psum.next(hold=True)
                                for hh in range(4):
                                    hs = slice(hh * 128, (hh + 1) * 128)
                                    P.mm(po[:, hs], vtok[slot][t][:, hs], PT[t][:, hh, 0, :], start=True, stop=False,
                                         reads=[bvtok[slot][t], bPT[t]], writes=[bpo])
                                    P.mm(po[:, hs], vtok[slot][t][:, hs], PT[t][:, hh, 1, :], start=False,
                                         stop=False, reads=[bvtok[slot][t], bPT[t]], writes=[bpo])
                                    P.mm(po[:, hs], SS[:, n, hs], qt[slot][:, hh, tsl], start=False, stop=True,
                                         reads=[bSSf[n], bSSb[n], bqt[slot]], writes=[bpo])
                                P.act(osq[:], po[:], AF.Square, reads=[bpo], writes=[bosq])
                                po_l.append((po, bpo))
                                yield
                                pn, bpn = psum.next()
                                P.mm(pn[:, :], ones_b[:], osq[:], reads=[bosq, bconst], writes=[bpn])
                                P.act(ro[:], pn[:], AF.Ln, reads=[bpn], writes=[bro], scale=1.0 / 128, bias=EPS)
                                P.act(ro[:], ro[:], AF.Exp, reads=[bro], writes=[bro], scale=-0.5)
                                P.tt("dve", om[:], po[:], ro[:], ALU.mult, reads=[bpo, bro], writes=[bom])
                                psum.release(bpo)
                                P.tt("pool", mT[:, 0:4, tsl], om[:].rearrange("p (h c) -> p h c", h=4),
                                     sg[slot][:, :, tsl], ALU.mult, reads=[bom, bsg[slot]], pwrites=[bmT])
                                yield
                            for pr in range(4):
                                pq, bpq = psum.next()
                                for j in range(2):
                                    dmc = 2 * pr + j
                                    for mc in range(KC):
                                        P.mm(pq[:, j * G:(j + 1) * G], wout[:, mc, dmc * 128:(dmc + 1) * 128],
                                             mT[:, mc, :], start=(mc == 0), stop=(mc == KC - 1),
                                             reads=[bmT, bW], writes=[bpq])
                                hf = pr // 2
                                P.tt("dve", xo[:, 2 * (pr % 2):2 * (pr % 2) + 2, :],
                                     pq[:].rearrange("p (b t) -> p b t", b=2), xg[slot][:, 2 * pr:2 * pr + 2, :],
                                     ALU.add, reads=[bpq, bxg[slot][pr]], pwrites=[bxo])
                                if pr % 2 == 1:
                                    t0_ = s * seq + gi * G
                                    P.dma(xs[hf * 512:(hf + 1) * 512, t0_:t0_ + G].rearrange("(kc p) t -> p kc t", p=128),
                                          xo[:], f"stx{hf}", reads=[bxo], writes=[bxs_h[hf]])
                                    if hf == 1:
                                        bxs[s][gi].w = bxs_h[1].w
                                        bxs[s][gi].r = {}
                                        bxs[s][gi].pw = dict([bxs_h[0].w])
                                yield
                            if g_next2 is not None:
                                load_x(s, g_next2, slot)

                        order = list(range(NG - 1, -1, -1))
                        load_x(s, order[0], 0)
                        if NG > 1:
                            load_x(s, order[1], 1)
                        fgs = {oi: front_gen(oi % 2, "p2", s, order[oi]) for oi in range(NG)}
                        for n in (order[0] * TPG + 1, order[0] * TPG):
                            bwd_step(n)
                        drive(fgs[0])
                        drive(p2_A(order[0], 0, fgs[0]))
                        if NG > 1:
                            drive(fgs[1])
                        for oi, gi in enumerate(order):
                            slot = oi % 2
                            nxt_g = order[oi + 1] if oi + 1 < NG else None
                            if nxt_g is not None:
                                for n in (nxt_g * TPG + 1, nxt_g * TPG):
                                    bwd_step(n)
                            ga = p2_A(nxt_g, 1 - slot, fgs[oi + 1]) if nxt_g is not None else None
                            gb = p2_B(gi, slot, order[oi + 2] if oi + 2 < NG else None)
                            drive_pattern("BABABABABABAABBBBAAAAA", ga, gb)
                            if oi + 2 < NG:
                                drive(fgs[oi + 2])
                    P.barrier()
                    P.flush(sems)
                    if stop == 4:
                        return nc

            with contextlib.ExitStack() as ph:
                EP = ph.enter_context
                w1b = EP(sbt("w1b", [128, KC, DFF], BF16))
                w2b = EP(sbt("w2b", [128, FC, D], BF16))
                bW = Buf()
                with contextlib.ExitStack() as ld:
                    EL = ld.enter_context
                    NST = 3
                    stg = [EL(sbt(f"stgm{i}", [128, DFF], F32)) for i in range(NST)]
                    bstg = [Buf() for _ in range(NST)]
                    ceng = ("dve", "act", "dve", "act", "dve", "act", "pool", "dve")
                    n_ld = 0
                    for kc in range(KC):
                        sl = n_ld % NST
                        n_ld += 1
                        P.dma(stg[sl][:], w1[l * D + kc * 128:l * D + (kc + 1) * 128, :], f"ldw{sl}",
                              writes=[bstg[sl]])
                        gm = vecs[:, V_MLP + l * 8 + kc:V_MLP + l * 8 + kc + 1]
                        for qd in range(8):
                            eng = ceng[qd]
                            o_ = w1b[:, kc, qd * 512:(qd + 1) * 512]
                            i_ = stg[sl][:, qd * 512:(qd + 1) * 512]
                            if eng == "act":
                                P.act(o_, i_, AF.Copy, reads=[bstg[sl], bconst], writes=[bW], scale=gm)
                            else:
                                P.ts(eng, o_, i_, gm, ALU.mult, reads=[bstg[sl], bconst], pwrites=[bW])
                    for j in range(FC // 4):
                        sl = n_ld % NST
                        n_ld += 1
                        P.dma(stg[sl][:].rearrange("p (a n) -> p a n", a=4),
                              w2[l * DFF + j * 512:l * DFF + (j + 1) * 512, :].rearrange("(a p) n -> p a n", p=128),
                              f"ldw{sl}", writes=[bstg[sl]])
                        for qd in range(8):
                            eng = ceng[qd]
                            P.cp(eng, w2b[:, 4 * j + qd // 2, (qd % 2) * 512:(qd % 2 + 1) * 512],
                                 stg[sl][:, qd * 512:(qd + 1) * 512], reads=[bstg[sl]], pwrites=[bW])
                    P.barrier()
                    P.flush(sems)
                with contextlib.ExitStack() as cs:
                    EC = cs.enter_context
                    xg = [EC(sbt(f"xm{i}", [128, KC, G], F32)) for i in range(2)]
                    bxg = [[Buf() for _ in range(4)] for _ in range(2)]
                    h2 = [EC(sbt(f"h2_{i}", [128, KC, G], BF16)) for i in range(2)]
                    bh2 = [Buf(), Buf()]
                    sq = EC(sbt("sqm", [128, KC, G], BF16)); bsq = Buf()
                    sqs = EC(sbt("sqsm", [128, G], BF16)); bsqs = Buf()
                    lnt = EC(sbt("lntm", [128, G], F32)); blnt = Buf()
                    rstd = EC(sbt("rstdm", [128, G], F32)); brstd = Buf()
                    h1 = EC(sbt("h1", [128, FC, G], BF16))
                    bh1 = [Buf() for _ in range(FC // 2)]
                    rl = [EC(sbt(f"rl{i}", [128, 2 * G], F32)) for i in range(3)]
                    brl = [Buf() for _ in range(3)]

                    def _red(e):
                        with nc.allow_low_precision("8-term sum of squares feeding a bf16 matmul operand"):
                            return e.tensor_reduce(out=sqs[:], in_=sq[:].rearrange("p kc t -> p t kc"),
                                                   axis=AX.X, op=ALU.add)

                    def norm_early(slot):
                        P.act(sq[:], xg[slot][:], AF.Square, reads=bxg[slot], writes=[bsq])
                        P.op("dve", _red, reads=[bsq], writes=[bsqs])

                    def norm_late(slot):
                        ps, bps = psum.next()
                        P.mm(ps[:, 0:G], ones_b[:], sqs[:], reads=[bsqs, bconst], writes=[bps])
                        P.act(lnt[:], ps[:, 0:G], AF.Ln, reads=[bps], writes=[blnt], scale=1.0 / D, bias=EPS)
                        P.act(rstd[:], lnt[:], AF.Exp, reads=[blnt], writes=[brstd], scale=-0.5)

                    def norm_stats(slot):
                        norm_early(slot)
                        norm_late(slot)

                    def h2_mul(slot):
                        P.tt("dve", h2[slot][:], xg[slot][:], rstd[:].unsqueeze(1).broadcast_to([128, KC, G]),
                             ALU.mult, reads=bxg[slot] + [brstd], writes=[bh2[slot]])

                    def front_m(slot):
                        norm_stats(slot)
                        h2_mul(slot)

                    seq_groups = [(s, gi) for s in range(nseq) for gi in range(NG)]

                    def load_m(idx, slot):
                        s, gi = seq_groups[idx]
                        P.dma(xg[slot][:], xview(xs, s, gi), f"ldx{slot}", reads=[bxs[s][gi]], writes=bxg[slot])

                    def finish_final(pslot, ps_, pgi):
                        norm_late(pslot)
                        for kc in range(KC):
                            P.stt("dve", xg[pslot][:, kc, :], xg[pslot][:, kc, :],
                                  vecs[:, V_FIN + kc:V_FIN + kc + 1], rstd[:], ALU.mult, ALU.mult,
                                  reads=[bxg[pslot][kc // 2], brstd, bconst], writes=[bxg[pslot][kc // 2]])
                        P.dma(xview(yT, ps_, pgi), xg[pslot][:], f"stx{pslot}", reads=bxg[pslot],
                              writes=[bxs[ps_][pgi]])

                    pending = None
                    load_m(0, 0)
                    front_m(0)
                    nrl = 0
                    for idx, (s, gi) in enumerate(seq_groups):
                        slot = idx % 2
                        if not last and idx + 1 < len(seq_groups):
                            load_m(idx + 1, 1 - slot)
                        for pr in range(FC // 2):
                            pf, bpf = psum.next()
                            for j in range(2):
                                fc = 2 * pr + j
                                for kc in range(KC):
                                    P.mm(pf[:, j * G:(j + 1) * G], w1b[:, kc, fc * 128:(fc + 1) * 128],
                                         h2[slot][:, kc, :], start=(kc == 0), stop=(kc == KC - 1),
                                         reads=[bh2[slot], bW], writes=[bpf])
                            r_ = nrl % 3
                            nrl += 1
                            if pr % 2 == 0:
                                P.act(rl[r_][:], pf[:], AF.Relu, reads=[bpf], writes=[brl[r_]])
                            else:
                                P.ts("dve", rl[r_][:], pf[:], 0.0, ALU.max, reads=[bpf], writes=[brl[r_]])
                            P.tt("pool", h1[:, 2 * pr:2 * pr + 2, :], rl[r_][:].rearrange("p (b t) -> p b t", b=2),
                                 rl[r_][:].rearrange("p (b t) -> p b t", b=2), ALU.mult, reads=[brl[r_]],
                                 writes=[bh1[pr]])
                            if last and pr == 5:
                                if pending is not None:
                                    finish_final(*pending)
                                    pending = None
                                if idx + 1 < len(seq_groups):
                                    load_m(idx + 1, 1 - slot)
                            if idx + 1 < len(seq_groups):
                                if pr == (12 if last else 7):
                                    norm_early(1 - slot)
                                elif pr == (15 if last else 12):
                                    norm_late(1 - slot)
                                    h2_mul(1 - slot)
                        for pr in range(4):
                            pq, bpq = psum.next()
                            for j in range(2):
                                dmc = 2 * pr + j
                                for fc in range(FC):
                                    P.mm(pq[:, j * G:(j + 1) * G], w2b[:, fc, dmc * 128:(dmc + 1) * 128], h1[:, fc, :],
                                         start=(fc == 0), stop=(fc == FC - 1), reads=[bh1[fc // 2], bW],
                                         writes=[bpq])
                            P.tt("dve", xg[slot][:, 2 * pr:2 * pr + 2, :], pq[:].rearrange("p (b t) -> p b t", b=2),
                                 xg[slot][:, 2 * pr:2 * pr + 2, :], ALU.add, reads=[bpq, bxg[slot][pr]],
                                 writes=[bxg[slot][pr]])
                        if not last:
                            P.dma(xview(xs, s, gi), xg[slot][:], f"stx{slot}", reads=bxg[slot],
                                  writes=[bxs[s][gi]])
                        else:
                            norm_early(slot)
                            pending = (slot, s, gi)
                    if pending is not None:
                        finish_final(*pending)
                    P.barrier()
                    P.flush(sems)
    return nc


def _consts():
    j = np.arange(128)[:, None]
    c = np.arange(128)[None, :]
    s = np.float32(-1.0 / 16.0)
    cst = np.zeros((128, 1025), np.float32)
    cst[:, 0:128] = (j > c) * s
    cst[:, 128:256] = (j < c) * s
    cst[:, 256:384] = (j <= c) * s
    cst[:, 384:512] = (j >= c) * s
    cst[:, 512:640] = 1.0
    cst[:, 768:896] = (c >= j)
    cst[:, 896:1024] = (j > c)
    cst[:, 1024] = s
    return cst


def make_shared(depth, norm_mix_g, w_in, w_a2_fwd, b_a_fwd, w_a2_bwd, b_a_bwd, gla_norm_g, sgu_norm_g, w_s, b_s,
                w_out, norm_mlp_g, w_mlp1, w_mlp2, final_norm_g):
    f = lambda a: np.ascontiguousarray(np.asarray(a, dtype=np.float32))
    L = depth
    vecs = np.concatenate([
        f(norm_mix_g).reshape(L, 8, 128).transpose(2, 0, 1).reshape(128, 8 * L),
        f(norm_mlp_g).reshape(L, 8, 128).transpose(2, 0, 1).reshape(128, 8 * L),
        f(gla_norm_g).reshape(L, 4, 128).transpose(2, 0, 1).reshape(128, 4 * L),
        f(sgu_norm_g).reshape(L, 4, 128).transpose(2, 0, 1).reshape(128, 4 * L),
        f(final_norm_g).reshape(8, 128).T,
    ], axis=1)
    return {
        "w_in": f(w_in).reshape(L * D, DIN),
        "w_out": f(w_out).reshape(L * D, D),
        "w1": f(w_mlp1).reshape(L * D, DFF),
        "w2": f(w_mlp2).reshape(L * DFF, D),
        "wa2f": f(w_a2_fwd).reshape(L * 16, 256),
        "wa2b": f(w_a2_bwd).reshape(L * 16, 256),
        "baf": f(b_a_fwd).reshape(L, 256),
        "bab": f(b_a_bwd).reshape(L, 256),
        "wsT": f(np.asarray(w_s, np.float32).transpose(0, 1, 3, 2)).reshape(L * 4 * 128, 128),
        "bs": f(b_s).reshape(L, 512),
        "vecs": f(vecs),
        "cst": _consts(),
    }


def run(x, shared, depth, n_cores, nseq, stop=99):
    x = np.asarray(x, dtype=np.float32)
    B, S, _ = x.shape
    assert B == n_cores * nseq
    nc = build_program(depth, nseq, S, stop)
    in_maps = []
    for c in range(n_cores):
        xT = np.ascontiguousarray(x[c * nseq:(c + 1) * nseq].reshape(nseq * S, D).T)
        m = dict(shared)
        m["xT"] = xT
        in_maps.append(m)
    res = run_bass_kernel_spmd(nc, in_maps, core_ids=list(range(n_cores)))
    out = np.empty((B, S, D), np.float32)
    for c in range(n_cores):
        out[c * nseq:(c + 1) * nseq] = res.results[c]["yT"].T.reshape(nseq, S, D)
    return out


def kernel(x, norm_mix_g, w_in, w_a2_fwd, b_a_fwd, w_a2_bwd, b_a_bwd, gla_norm_g, sgu_norm_g, w_s, b_s, w_out,
           norm_mlp_g, w_mlp1, w_mlp2, final_norm_g):
    depth = 4
    shared = make_shared(depth, norm_mix_g, w_in, w_a2_fwd, b_a_fwd, w_a2_bwd, b_a_bwd, gla_norm_g, sgu_norm_g,
                         w_s, b_s, w_out, norm_mlp_g, w_mlp1, w_mlp2, final_norm_g)
    return run(x, shared, depth, 8, 2)
```

```python
import contextlib
import numpy as np
import concourse.bass as bass
import concourse.mybir as mybir
from concourse.bass_utils import run_bass_kernel_spmd

F32 = mybir.dt.float32
BF16 = mybir.dt.bfloat16
AF = mybir.ActivationFunctionType
ALU = mybir.AluOpType
AX = mybir.AxisListType

D = 1024
KC = 8
DIN = 2592
DFF = 4096
FC = 32
G = 256
TPG = 2
EPS = 1e-6
NWF = 16 * 128 + 32


class Buf:
    __slots__ = ("w", "r", "pw")

    def __init__(self):
        self.w = None
        self.r = {}
        self.pw = {}


class Prog:
    CE = ("pe", "act", "dve", "pool")

    def __init__(self, nc):
        self.nc = nc
        self.q = {e: [] for e in ("pe", "act", "dve", "pool", "sp")}
        self.n = {e: 0 for e in self.CE}
        self.waited = {e: {} for e in self.q}
        self.dcnt = {}
        self.ninst = 0

    def _wait(self, eng, key, val):
        if key == eng and eng == "pe":
            return
        if self.waited[eng].get(key, 0) >= val:
            return
        self.waited[eng][key] = val
        self.q[eng].append(("w", key, val))

    def _deps(self, eng, reads, writes, pwrites=()):
        for b in reads:
            if b.w is not None:
                self._wait(eng, *b.w)
            for k, v in b.pw.items():
                self._wait(eng, k, v)
        for b in writes:
            if b.w is not None:
                self._wait(eng, *b.w)
            for k, v in b.pw.items():
                self._wait(eng, k, v)
            for k, v in b.r.items():
                self._wait(eng, k, v)
        for b in pwrites:
            if b.w is not None:
                self._wait(eng, *b.w)
            for k, v in b.r.items():
                self._wait(eng, k, v)

    @staticmethod
    def _mark(tok, reads, writes, pwrites=()):
        k, v = tok
        for b in pwrites:
            if b.r:
                b.r = {}
                b.pw = {}
                b.w = None
            if b.pw.get(k, 0) < v:
                b.pw[k] = v
        for b in reads:
            if b.r.get(k, 0) < v:
                b.r[k] = v
        for b in writes:
            b.w = tok
            b.r = {}
            b.pw = {}

    def op(self, eng, fn, reads=(), writes=(), pwrites=()):
        self._deps(eng, reads, writes, pwrites)
        self.n[eng] += 1
        tok = (eng, self.n[eng])
        self.q[eng].append(("o", fn))
        self._mark(tok, reads, writes, pwrites)
        self.ninst += 1
        return tok

    def dma(self, out_ap, in_ap, semkey, reads=(), writes=(), eng="sp", pwrites=()):
        self._deps(eng, reads, writes, pwrites)
        c = self.dcnt.get(semkey, 0) + 16
        self.dcnt[semkey] = c
        tok = (semkey, c)
        self.q[eng].append(("d", out_ap, in_ap, semkey))
        self._mark(tok, reads, writes, pwrites)
        self.ninst += 1
        return tok

    def barrier(self):
        for e in self.q:
            for k, v in self.n.items():
                if v:
                    self._wait(e, k, v) if k != e else None
            for k, v in self.dcnt.items():
                self._wait(e, k, v)

    def flush(self, sems):
        nc = self.nc
        q = self.q

        def run(engine, items, ekey):
            for it in items:
                if it[0] == "w":
                    engine.wait_ge(sems[it[1]], it[2])
                elif it[0] == "o":
                    it[1](engine).then_inc(sems[ekey], 1)
                else:
                    engine.dma_start(out=it[1], in_=it[2]).then_inc(sems[it[3]], 16)

        with nc.Block() as block:
            @block.tensor
            def _(e):
                run(e, q["pe"], "pe")

            @block.scalar
            def _(e):
                run(e, q["act"], "act")

            @block.vector
            def _(e):
                run(e, q["dve"], "dve")

            @block.gpsimd
            def _(e):
                run(e, q["pool"], "pool")

            @block.sync
            def _(e):
                run(e, q["sp"], "sp")
        for k in q:
            q[k] = []

    def mm(self, out, lhsT, rhs, start=True, stop=True, reads=(), writes=()):
        return self.op("pe", lambda e: e.matmul(out, lhsT=lhsT, rhs=rhs, start=start, stop=stop), reads, writes)

    def act(self, out, in_, func, reads=(), writes=(), scale=1.0, bias=0.0, pwrites=()):
        return self.op("act", lambda e: e.activation(out=out, in_=in_, func=func, bias=bias, scale=scale),
                       reads, writes, pwrites)

    def tt(self, eng, out, in0, in1, op, reads=(), writes=(), pwrites=()):
        return self.op(eng, lambda e: e.tensor_tensor(out=out, in0=in0, in1=in1, op=op), reads, writes, pwrites)

    def ts(self, eng, out, in0, s1, op0, reads=(), writes=(), s2=None, op1=None, pwrites=()):
        if op1 is None:
            return self.op(eng, lambda e: e.tensor_scalar(out=out, in0=in0, scalar1=s1, scalar2=None, op0=op0),
                           reads, writes, pwrites)
        return self.op(eng, lambda e: e.tensor_scalar(out=out, in0=in0, scalar1=s1, scalar2=s2, op0=op0, op1=op1),
                       reads, writes, pwrites)

    def stt(self, eng, out, in0, scalar, in1, op0, op1, reads=(), writes=(), pwrites=()):
        return self.op(eng, lambda e: e.scalar_tensor_tensor(out=out, in0=in0, scalar=scalar, in1=in1,
                                                             op0=op0, op1=op1), reads, writes, pwrites)

    def cp(self, eng, out, in_, reads=(), writes=(), pwrites=()):
        if eng == "act":
            return self.act(out, in_, AF.Copy, reads, writes, pwrites=pwrites)
        return self.op(eng, lambda e: e.tensor_copy(out=out, in_=in_), reads, writes, pwrites)

    def memset(self, eng, ap, val, writes=()):
        return self.op(eng, lambda e: e.memset(ap, val), (), writes)


class PsumRing:
    def __init__(self, banks):
        self.banks = banks
        self.bufs = [Buf() for _ in banks]
        self.held = set()
        self.i = 0

    def next(self, hold=False):
        for _ in range(len(self.banks)):
            i = self.i
            self.i = (i + 1) % len(self.banks)
            if i not in self.held:
                if hold:
                    self.held.add(i)
                return self.banks[i], self.bufs[i]
        raise RuntimeError("all PSUM banks held")

    def release(self, buf):
        self.held.discard(self.bufs.index(buf))


def build_program(depth, nseq, seq, stop=99):
    NT = nseq * seq
    NG = seq // G
    NCH = seq // 128
    nc = bass.Bass("TRN2", target_bir_lowering=False)
    _uid = [0]

    def sbt(name, shape, dt):
        _uid[0] += 1
        return nc.sbuf_tensor(f"{name}_u{_uid[0]}", shape, dt)

    def din(name, shape):
        return nc.dram_tensor(name, list(shape), F32, kind="ExternalInput").ap()

    xT = din("xT", [D, NT])
    w_in = din("w_in", [depth * D, DIN])
    w_out = din("w_out", [depth * D, D])
    w1 = din("w1", [depth * D, DFF])
    w2 = din("w2", [depth * DFF, D])
    wa2f = din("wa2f", [depth * 16, 256])
    wa2b = din("wa2b", [depth * 16, 256])
    baf = din("baf", [depth, 256])
    bab = din("bab", [depth, 256])
    wsT = din("wsT", [depth * 4 * 128, 128])
    bsd = din("bs", [depth, 512])
    vecs_d = din("vecs", [128, 24 * depth + 8])
    cst_d = din("cst", [128, 1025])
    yT = nc.dram_tensor("yT", [D, NT], F32, kind="ExternalOutput").ap()
    xs = yT
    vscr = nc.dram_tensor("vscr", [nseq * NCH, 128, 512], BF16).ap()
    lscr = nc.dram_tensor("lscr", [nseq * NCH, 128, 512], BF16).ap()

    V_MIX = 0
    V_MLP = 8 * depth
    V_GLA = 16 * depth
    V_SGU = 20 * depth
    V_FIN = 24 * depth

    with contextlib.ExitStack() as top:
        E = top.enter_context
        sem_names = ["pe", "act", "dve", "pool", "ldx0", "ldx1", "stx0", "stx1", "ldw0", "ldw1", "ldw2", "ldc", "ldk",
                     "sl00", "sl01", "sl10", "sl11", "sv0", "sv1", "ll00", "ll01", "ll10", "ll11",
                     "lv00", "lv01", "lv10", "lv11"]
        sems = {k: E(nc.semaphore(k)) for k in sem_names}
        P = Prog(nc)
        psum = PsumRing([E(nc.psum_tensor(f"ps{i}", [128, 512], F32)) for i in range(8)])

        cst = E(sbt("cst", [128, 1025], F32))
        vecs = E(sbt("vecs", [128, 24 * depth + 8], F32))
        ugt_b = E(sbt("ugt_b", [128, 128], BF16))
        ult_b = E(sbt("ult_b", [128, 128], BF16))
        ucat_b = E(sbt("ucat_b", [128, 256], BF16))
        ones_b = E(sbt("ones_b", [128, 128], BF16))
        negc_b = E(sbt("negc_b", [128, 2], BF16))
        bconst = Buf()
        bcst = Buf()
        P.dma(cst[:], cst_d[:, :], "ldk", writes=[bcst])
        t = P.dma(vecs[:], vecs_d[:, :], "ldk", writes=[bconst])
        bcst.w = t
        maskcat = cst[:, 768:1024]
        P.cp("dve", ugt_b[:], cst[:, 0:128], reads=[bcst], writes=[bconst])
        P.cp("dve", ult_b[:], cst[:, 128:256], reads=[bcst], writes=[bconst])
        P.cp("dve", ucat_b[:], cst[:, 256:512], reads=[bcst], writes=[bconst])
        P.cp("dve", ones_b[:], cst[:, 512:640], reads=[bcst], writes=[bconst])
        P.cp("dve", negc_b[:, 0:1], cst[:, 1024:1025], reads=[bcst], writes=[bconst])

        bxs = [[Buf() for _ in range(NG)] for _ in range(nseq)]
        bvs = [[Buf() for _ in range(NCH)] for _ in range(nseq)]
        bls = [[Buf() for _ in range(NCH)] for _ in range(nseq)]

        def xview(t_ap, s, gi):
            t0 = s * seq + gi * G
            return t_ap[:, t0:t0 + G].rearrange("(kc p) t -> p kc t", p=128)

        for l in range(depth):
            last = (l == depth - 1)
            with contextlib.ExitStack() as ph:
                EP = ph.enter_context
                winf = EP(sbt("winf", [128, KC, NWF], BF16))
                wint = EP(sbt("wint", [128, KC, 1024], BF16))
                wink = EP(sbt("wink", [128, KC, 256], BF16))
                wout = EP(sbt("wout", [128, KC, D], BF16))
                waug_b = EP(sbt("waug_b", [33, 512], BF16))
                wst_b = EP(sbt("wst_b", [128, 4, 128], BF16))
                bs_bc = EP(sbt("bs_bc", [128, 512], F32))
                SS = EP(sbt("SS", [128, NCH, 512], BF16))
                dec = EP(sbt("dec", [128, NCH, 4], F32))
                bW = Buf()
                with contextlib.ExitStack() as ld:
                    EL = ld.enter_context
                    stg = [EL(sbt(f"stg{i}", [128, DIN], F32)) for i in range(2)]
                    bstg = [Buf(), Buf()]
                    waug_f = EL(sbt("waug_f", [33, 512], F32))
                    wst_f = EL(sbt("wst_f", [128, 4, 128], F32))
                    bsm = Buf()
                    P.memset("pool", waug_f[:], 0.0, writes=[bsm])
                    wv = waug_f[:].rearrange("p (h d k) -> p h d k", h=4, d=2)
                    P.dma(wv[0:16, :, 0, :], wa2f[l * 16:(l + 1) * 16, :].rearrange("p (h k) -> p h k", h=4),
                          "ldc", writes=[bsm])
                    P.dma(wv[16:32, :, 1, :], wa2b[l * 16:(l + 1) * 16, :].rearrange("p (h k) -> p h k", h=4),
                          "ldc", writes=[bsm])
                    P.dma(wv[32:33, :, 0, :], baf[l:l + 1, :].rearrange("p (h k) -> p h k", h=4), "ldc", writes=[bsm])
                    P.dma(wv[32:33, :, 1, :], bab[l:l + 1, :].rearrange("p (h k) -> p h k", h=4), "ldc", writes=[bsm])
                    P.dma(wst_f[:], wsT[l * 512:(l + 1) * 512, :].rearrange("(g q) p -> q g p", g=4), "ldc",
                          writes=[bsm])
                    t = P.dma(bs_bc[:], bsd[l:l + 1, :].partition_broadcast(128), "ldc", writes=[bsm], pwrites=[bW])
                    bsm.w = t
                    P.cp("dve", waug_b[:], waug_f[:], reads=[bsm], pwrites=[bW])
                    P.cp("pool", wst_b[:], wst_f[:], reads=[bsm], pwrites=[bW])
                    rr = 0
                    engs = ("dve", "pool")
                    for kc in range(KC):
                        sl = kc % 2
                        P.dma(stg[sl][:], w_in[l * D + kc * 128:l * D + (kc + 1) * 128, :], f"ldw{sl}",
                              writes=[bstg[sl]])
                        gm = vecs[:, V_MIX + l * 8 + kc:V_MIX + l * 8 + kc + 1]
                        s_ = stg[sl]
                        fv = winf[:, kc, 0:1024].rearrange("p (b d k) -> p b d k", b=8, d=2)
                        jobs = []
                        for d_ in range(2):
                            jobs.append((fv[:, 0:4, d_, :], s_[:, 0:256].rearrange("p (h k) -> p h k", h=4), 0.125))
                            jobs.append((fv[:, 4:8, d_, :], s_[:, 256:512].rearrange("p (h k) -> p h k", h=4), None))
                        jobs.append((wink[:, kc, :], s_[:, 256:512], None))
                        jobs.append((wint[:, kc, 0:512], s_[:, 512:1024], None))
                        jobs.append((winf[:, kc, 1024:1536], s_[:, 1024:1536], None))
                        jobs.append((winf[:, kc, 2048:2080], s_[:, 1536:1568], None))
                        jobs.append((winf[:, kc, 1536:2048], s_[:, 1568:2080], None))
                        jobs.append((wint[:, kc, 512:1024], s_[:, 2080:2592], None))
                        jeng = ("dve", "act", "dve", "pool", "act", "dve", "act", "pool", "dve", "act")
                        for ji, (o_, i_, sc) in enumerate(jobs):
                            eng = jeng[ji]
                            if sc is not None:
                                P.ts("dve", o_, i_, gm, ALU.mult, reads=[bstg[sl], bconst], pwrites=[bW],
                                     s2=sc, op1=ALU.mult)
                            elif eng == "act":
                                P.act(o_, i_, AF.Copy, reads=[bstg[sl], bconst], writes=[bW], scale=gm)
                            else:
                                P.ts(eng, o_, i_, gm, ALU.mult, reads=[bstg[sl], bconst], pwrites=[bW])
                    for kc in range(KC):
                        sl = kc % 2
                        P.dma(stg[sl][:, 0:D], w_out[l * D + kc * 128:l * D + (kc + 1) * 128, :], f"ldw{sl}",
                              writes=[bstg[sl]])
                        for hf in range(2):
                            eng = ("dve", "act")[hf]
                            o_ = wout[:, kc, hf * 512:(hf + 1) * 512]
                            i_ = stg[sl][:, hf * 512:(hf + 1) * 512]
                            if kc < 4:
                                gg = vecs[:, V_GLA + l * 4 + kc:V_GLA + l * 4 + kc + 1]
                                if eng == "act":
                                    P.act(o_, i_, AF.Copy, reads=[bstg[sl], bconst], writes=[bW], scale=gg)
                                else:
                                    P.ts(eng, o_, i_, gg, ALU.mult, reads=[bstg[sl], bconst], pwrites=[bW])
                            else:
                                P.cp(eng, o_, i_, reads=[bstg[sl]], pwrites=[bW])
                    P.barrier()
                    P.flush(sems)
                    if stop == 1:
                        return nc

                with contextlib.ExitStack() as cs:
                    EC = cs.enter_context
                    xg = [EC(sbt(f"xg{i}", [128, KC, G], F32)) for i in range(2)]
                    bxg = [[Buf() for _ in range(4)] for _ in range(2)]
                    hb = [EC(sbt(f"hb{i}", [128, KC, G], BF16)) for i in range(2)]
                    bh = [[Buf(), Buf()] for _ in range(2)]
                    sq = EC(sbt("sq", [128, KC, G], BF16)); bsq = Buf()
                    sqs = EC(sbt("sqs", [128, G], BF16)); bsqs = Buf()
                    lnt = EC(sbt("lnt", [128, G], F32)); blnt = Buf()
                    rstd = EC(sbt("rstd", [128, G], F32)); brstd = Buf()
                    aaug = [EC(sbt(f"aaug{i}", [33, G], BF16)) for i in range(2)]
                    baaug = [Buf(), Buf()]
                    esb = EC(sbt("esb", [128, 512], F32)); besb = Buf()
                    lsb = [[EC(sbt(f"lsb{i}_{t}", [128, 512], BF16)) for t in range(TPG)] for i in range(2)]
                    blsb = [[Buf() for _ in range(TPG)] for _ in range(2)]
                    edsb = [EC(sbt(f"edsb{i}", [128, 512], BF16)) for i in range(TPG)]; bedsb = [Buf(), Buf()]
                    khat = [EC(sbt(f"khat{i}", [128, 512], BF16)) for i in range(TPG)]; bkhat = [Buf(), Buf()]
                    vtok = [[EC(sbt(f"vtok{i}_{t}", [128, 512], BF16)) for t in range(TPG)] for i in range(2)]
                    bvtok = [[Buf() for _ in range(TPG)] for _ in range(2)]
                    Sst = [EC(sbt(f"Sst{i}", [128, 512], F32)) for i in range(2)]
                    Eb = EC(sbt("Eb", [128, 4, G], BF16)); bEb = Buf()
                    Einv = EC(sbt("Einv", [128, 4, G], BF16)); bEinv = Buf()
                    qt = [EC(sbt(f"qt{i}", [128, 4, G], BF16)) for i in range(2)]; bqt = [Buf(), Buf()]
                    kt = [EC(sbt(f"kt{i}", [128, 4, G], BF16)) for i in range(2)]; bkt = [Buf(), Buf()]
                    sg = [EC(sbt(f"sg{i}", [128, 4, G], BF16)) for i in range(2)]; bsg = [Buf(), Buf()]
                    gu = [EC(sbt(f"gu{i}", [128, 4, G], BF16)) for i in range(2)]; bgu = [Buf(), Buf()]
                    gsvt = [EC(sbt(f"gsv{i}", [128, 512], F32)) for i in range(TPG)]; bgsvt = [Buf(), Buf()]
                    ss4b = EC(sbt("ss4b", [128, 8], F32)); bss4b = Buf()
                    sqv = esb; bsqv = besb
                    ss4 = EC(sbt("ss4", [128, 8], F32)); bss4 = Buf()
                    vn = [[EC(sbt(f"vn{i}_{t}", [128, 4, 128], BF16)) for t in range(TPG)] for i in range(2)]
                    bvn = [[Buf() for _ in range(TPG)] for _ in range(2)]
                    PT = [EC(sbt(f"PT{i}", [128, 4, 2, 128], BF16)) for i in range(TPG)]
                    bPT = [Buf() for _ in range(TPG)]
                    osq = EC(sbt("osq", [128, 512], BF16)); bosq = Buf()
                    ro = EC(sbt("ro", [128, 512], F32)); bro = Buf()
                    om = EC(sbt("om", [128, 512], F32)); bom = Buf()
                    sgt = EC(sbt("sgt", [128, 4, 128], F32)); bsgt = Buf()
                    mT = EC(sbt("mT", [128, KC, G], BF16)); bmT = Buf()
                    xo = EC(sbt("xo", [128, 4, G], F32)); bxo = Buf()
                    bxs_h = [Buf(), Buf()]

                    for i in range(2):
                        P.memset("pool", aaug[i][32:33, :], 1.0, writes=[baaug[i]])

                    def load_x(s, gi, slot):
                        if l == 0:
                            P.dma(xg[slot][:], xview(xT, s, gi), f"ldx{slot}", writes=bxg[slot])
                        else:
                            P.dma(xg[slot][:], xview(xs, s, gi), f"ldx{slot}", reads=[bxs[s][gi]],
                                  writes=bxg[slot])

                    def front_gen(slot, mode, s_, gi_):
                        P.act(sq[:], xg[slot][:], AF.Square, reads=bxg[slot], writes=[bsq])
                        yield
                        ps, bps = psum.next()
                        for kc in range(KC):
                            P.mm(ps[:, 0:G], ones_b[:], sq[:, kc, :], start=(kc == 0), stop=(kc == KC - 1),
                                 reads=[bsq, bconst], writes=[bps])
                        P.act(lnt[:], ps[:, 0:G], AF.Ln, reads=[bps], writes=[blnt], scale=1.0 / D, bias=EPS)
                        P.act(rstd[:], lnt[:], AF.Exp, reads=[blnt], writes=[brstd], scale=-0.5)
                        for hf, eng, k0_, k1_ in ((0, "dve", 0, 6), (1, "pool", 6, 8)):
                            P.tt(eng, hb[slot][:, k0_:k1_, :], xg[slot][:, k0_:k1_, :],
                                 rstd[:].unsqueeze(1).broadcast_to([128, k1_ - k0_, G]), ALU.mult,
                                 reads=bxg[slot] + [brstd], writes=[bh[slot][hf]])
                        yield
                        if mode == "p2":
                            for t in range(TPG):
                                n_ = s_ * NCH + gi_ * TPG + t
                                P.dma(lsb[slot][t][:], lscr[n_], f"ll{slot}{t}", reads=[bls[s_][gi_ * TPG + t]],
                                      writes=[blsb[slot][t]])
                                P.dma(vtok[slot][t][:], vscr[n_], f"lv{slot}{t}", reads=[bvs[s_][gi_ * TPG + t]],
                                      writes=[bvtok[slot][t]])
                            yield
                            yield
                            return
                        pa, bpa = psum.next()
                        for kc in range(KC):
                            P.mm(pa[0:32, 0:G], winf[:, kc, 2048:2080], hb[slot][:, kc, :], start=(kc == 0),
                                 stop=(kc == KC - 1), reads=[bh[slot][kc // 6], bW], writes=[bpa])
                        P.act(aaug[slot][0:32, :], pa[0:32, 0:G], AF.Copy, reads=[bpa], writes=[baaug[slot]])
                        yield
                        for t in range(TPG):
                            pl, bpl = psum.next()
                            P.mm(pl[:, :], aaug[slot][0:33, t * 128:(t + 1) * 128], waug_b[0:33, :],
                                 reads=[baaug[slot], bW], writes=[bpl])
                            P.act(esb[:], pl[:], AF.Exp, reads=[bpl], writes=[besb], scale=-1.0)
                            P.act(lsb[slot][t][:], esb[:], AF.Ln, reads=[besb], writes=[blsb[slot][t]], bias=1.0)
                            n_ = s_ * NCH + gi_ * TPG + t
                            P.dma(lscr[n_], lsb[slot][t][:], f"sl{slot}{t}", reads=[blsb[slot][t]],
                                  writes=[bls[s_][gi_ * TPG + t]])
                        yield

                    def drive_pattern(pattern, ga, gb):
                        for ch in pattern:
                            g = ga if ch == "A" else gb
                            if g is not None:
                                next(g, None)
                        drive(ga, gb)

                    def drive(*gens):
                        gens = [g for g in gens if g is not None]
                        while gens:
                            for g in list(gens):
                                try:
                                    next(g)
                                except StopIteration:
                                    gens.remove(g)

                    for s in range(nseq):
                        bSSf = [Buf() for _ in range(NCH)]
                        bSSb = [Buf() for _ in range(NCH)]
                        bdec = [Buf() for _ in range(NCH)]
                        bS = [[[Buf() for _ in range(4)] for _ in range(2)] for _ in range(2)]
                        P.memset("dve", Sst[0][0:64, :], 0.0, writes=bS[0][0])
                        P.memset("pool", Sst[0][64:128, :], 0.0, writes=bS[1][0])

                        def fwd_step(n):
                            cur, nxt = Sst[n % 2], Sst[(n + 1) % 2]
                            for hh in range(4):
                                hs = slice(hh * 128, (hh + 1) * 128)
                                P.stt("dve", nxt[0:64, hs], cur[0:64, hs], dec[0:64, n, hh:hh + 1], SS[0:64, n, hs],
                                      ALU.mult, ALU.add, reads=[bS[0][n % 2][hh], bdec[n], bSSf[n]],
                                      writes=[bS[0][(n + 1) % 2][hh]])
                            P.cp("pool", SS[0:64, n, :], cur[0:64, :], reads=bS[0][n % 2], writes=[bSSf[n]])

                        def bwd_step(n):
                            i = NCH - 1 - n
                            cur, nxt = Sst[i % 2], Sst[(i + 1) % 2]
                            P.tt("pool", nxt[64:128, :].rearrange("p (h v) -> p h v", h=4),
                                 cur[64:128, :].rearrange("p (h v) -> p h v", h=4),
                                 dec[64:128, n, :].unsqueeze(2).broadcast_to([64, 4, 128]), ALU.mult,
                                 reads=bS[1][i % 2] + [bdec[n]], writes=bS[1][(i + 1) % 2])
                            P.tt("pool", nxt[64:128, :], nxt[64:128, :], SS[64:128, n, :], ALU.add,
                                 reads=bS[1][(i + 1) % 2] + [bSSb[n]], writes=bS[1][(i + 1) % 2])
                            P.cp("pool", SS[64:128, n, :], cur[64:128, :], reads=bS[1][i % 2], writes=[bSSb[n]])

                        def p1_tiles(gi, slot):
                            banks = []
                            for t in range(TPG):
                                tsl = slice(t * 128, (t + 1) * 128)
                                pk, bpk = psum.next(hold=True)
                                pv, bpv = psum.next(hold=True)
                                for kc in range(KC):
                                    P.mm(pk[:, 0:256], hb[slot][:, kc, tsl], wink[:, kc, :], start=(kc == 0),
                                         stop=(kc == KC - 1), reads=[bh[slot][kc // 6], bW], writes=[bpk])
                                    P.mm(pv[:, :], hb[slot][:, kc, tsl], wint[:, kc, 0:512], start=(kc == 0),
                                         stop=(kc == KC - 1), reads=[bh[slot][kc // 6], bW], writes=[bpv])
                                P.cp("act", vtok[0][t][:], pv[:], reads=[bpv], writes=[bvtok[0][t]])
                                psum.release(bpv)
                                P.dma(vscr[s * NCH + gi * TPG + t], vtok[0][t][:], f"sv{t}", reads=[bvtok[0][t]],
                                      writes=[bvs[s][gi * TPG + t]])
                                banks.append((pk, bpk, pv, bpv))
                                yield
                            for t in range(TPG):
                                n = gi * TPG + t
                                pk, bpk, pv, bpv = banks[t]
                                pd, bpd = psum.next()
                                l4 = lsb[slot][t][:].rearrange("p (h d k) -> p h d k", h=4, d=2)
                                pd4 = pd[:].rearrange("p (h d k) -> p h d k", h=4, d=2)
                                P.mm(pd4[:, :, 0, :], ugt_b[:], l4[:, :, 0, :], reads=[blsb[slot][t], bconst],
                                     writes=[bpd])
                                P.mm(pd4[:, :, 1, :], ult_b[:], l4[:, :, 1, :], reads=[blsb[slot][t], bconst],
                                     writes=[bpd])
                                pdc, bpdc = psum.next()
                                for hh in range(4):
                                    hs = slice(hh * 128, (hh + 1) * 128)
                                    P.mm(pdc[:, hh:hh + 1], lsb[slot][t][:, hs], negc_b[:, 0:1],
                                         reads=[blsb[slot][t], bconst], writes=[bpdc])
                                P.act(edsb[t][:], pd[:], AF.Exp, reads=[bpd], writes=[bedsb[t]])
                                P.act(dec[:, n, :], pdc[:, 0:4], AF.Exp, reads=[bpdc], writes=[bdec[n]])
                                k4 = pk[:, 0:256].rearrange("p (h k) -> p h k", h=4).unsqueeze(2).broadcast_to(
                                    [128, 4, 2, 64])
                                P.tt("dve", khat[t][:].rearrange("p (h d k) -> p h d k", h=4, d=2),
                                     edsb[t][:].rearrange("p (h d k) -> p h d k", h=4, d=2), k4, ALU.mult,
                                     reads=[bpk, bedsb[t]], writes=[bkhat[t]])
                                psum.release(bpk)
                                yield
                            for t in range(TPG):
                                n = gi * TPG + t
                                pkv, bpkv = psum.next()
                                for hh in range(4):
                                    hs = slice(hh * 128, (hh + 1) * 128)
                                    P.mm(pkv[:, hs], khat[t][:, hs], vtok[0][t][:, hs],
                                         reads=[bkhat[t], bvtok[0][t]], writes=[bpkv])
                                P.cp("dve", SS[:, n, :], pkv[:], reads=[bpkv], writes=[bSSf[n], bSSb[n]])
                                fwd_step(n)
                                yield

                        load_x(s, 0, 0)
                        if NG > 1:
                            load_x(s, 1, 1)
                        fgs = {g: front_gen(g % 2, "p1", s, g) for g in range(NG)}
                        drive(fgs[0])
                        if NG > 1:
                            next(fgs[1])
                        for gi in range(NG):
                            slot = gi % 2
                            if gi + 2 < NG:
                                load_x(s, gi + 2, slot)
                            fa = fgs.get(gi + 1)
                            fb = fgs.get(gi + 2)
                            tg = p1_tiles(gi, slot)
                            for ch in "ABBBBACBBA":
                                g = {"A": fa, "B": tg, "C": fb}[ch]
                                if g is not None:
                                    next(g, None)
                            drive(tg, fa)

                        def p2_A(gi, slot, fg):
                            def tok(t):
                                tsl = slice(t * 128, (t + 1) * 128)
                                pw, bpw = psum.next()
                                for kc in range(KC):
                                    P.mm(pw[:, :], hb[slot][:, kc, tsl], wint[:, kc, 512:1024], start=(kc == 0),
                                         stop=(kc == KC - 1), reads=[bh[slot][kc // 6], bW], writes=[bpw])
                                P.act(gsvt[t][:], pw[:], AF.Gelu, reads=[bpw], writes=[bgsvt[t]])
                                P.tt("pool", sqv[:], gsvt[t][:], gsvt[t][:], ALU.mult, reads=[bgsvt[t]], writes=[bsqv])
                                P.op("dve", lambda e: e.tensor_reduce(
                                    out=ss4[:, 4 * t:4 * t + 4], in_=sqv[:].rearrange("p (g c) -> p g c", g=4),
                                    axis=AX.X, op=ALU.add), reads=[bsqv], pwrites=[bss4])

                            def vnorm():
                                P.act(ss4b[:], ss4[:], AF.Ln, reads=[bss4], writes=[bss4b], scale=1.0 / 128, bias=EPS)
                                P.act(ss4b[:], ss4b[:], AF.Exp, reads=[bss4b], writes=[bss4b], scale=-0.5)
                                for t in range(TPG):
                                    P.tt("dve", vn[slot][t][:], gsvt[t][:].rearrange("p (g c) -> p g c", g=4),
                                         ss4b[:, 4 * t:4 * t + 4].unsqueeze(2).broadcast_to([128, 4, 128]), ALU.mult,
                                         reads=[bgsvt[t], bss4b], writes=[bvn[slot][t]])

                            def decays():
                                for t in range(TPG):
                                    tsl = slice(t * 128, (t + 1) * 128)
                                    for bi in range(2):
                                        pb, bpb = psum.next()
                                        for j in range(2):
                                            hh = 2 * bi + j
                                            P.mm(pb[:, j * 256:(j + 1) * 256],
                                                 lsb[slot][t][:, hh * 128:(hh + 1) * 128], ucat_b[:, :],
                                                 reads=[blsb[slot][t], bconst], writes=[bpb])
                                        pbv = pb[:].rearrange("p (h c) -> p h c", h=2)
                                        P.act(Eb[0:64, 2 * bi:2 * bi + 2, tsl], pbv[0:64, :, 0:128], AF.Exp,
                                              reads=[bpb], pwrites=[bEb])
                                        P.act(Eb[64:128, 2 * bi:2 * bi + 2, tsl], pbv[64:128, :, 128:256], AF.Exp,
                                              reads=[bpb], pwrites=[bEb])
                                        P.act(Einv[0:64, 2 * bi:2 * bi + 2, tsl], pbv[0:64, :, 0:128], AF.Exp,
                                              reads=[bpb], pwrites=[bEinv], scale=-1.0)
                                        P.act(Einv[64:128, 2 * bi:2 * bi + 2, tsl], pbv[64:128, :, 128:256], AF.Exp,
                                              reads=[bpb], pwrites=[bEinv], scale=-1.0)

                            def feat(pr):
                                pf, bpf = psum.next()
                                for j in range(2):
                                    blk = 2 * pr + j
                                    for kc in range(KC):
                                        P.mm(pf[:, j * G:(j + 1) * G], winf[:, kc, blk * 128:(blk + 1) * 128],
                                             hb[slot][:, kc, :], start=(kc == 0), stop=(kc == KC - 1),
                                             reads=[bh[slot][kc // 6], bW], writes=[bpf])
                                pfv = pf[:].rearrange("p (b t) -> p b t", b=2)
                                b2 = slice(2 * (pr % 2), 2 * (pr % 2) + 2)
                                if pr < 2:
                                    P.tt("dve", qt[slot][:, b2, :], pfv, Eb[:, b2, :], ALU.mult, reads=[bpf, bEb],
                                         pwrites=[bqt[slot]])
                                elif pr < 4:
                                    P.tt("dve", kt[slot][:, b2, :], pfv, Einv[:, b2, :], ALU.mult,
                                         reads=[bpf, bEinv], pwrites=[bkt[slot]])
                                elif pr < 6:
                                    P.act(sg[slot][:, b2, :], pfv, AF.Silu, reads=[bpf], pwrites=[bsg[slot]])
                                else:
                                    P.act(gu[slot][:, b2, :], pfv, AF.Gelu, reads=[bpf], pwrites=[bgu[slot]])

                            next(fg)
                            yield
                            next(fg)
                            yield
                            next(fg)
                            yield
                            tok(0)
                            yield
                            tok(1)
                            yield
                            decays()
                            yield
                            for pr in (4, 5, 6, 7, 0, 1, 2, 3):
                                feat(pr)
                                yield
                            vnorm()
                            yield

                        def p2_B(gi, slot, g_next2=None):
                            po_l, pm_l = [], []
                            for t in range(TPG):
                                n = gi * TPG + t
                                tsl = slice(t * 128, (t + 1) * 128)
                                psf, bpsf = psum.next()
                                psb, bpsb = psum.next()
                                for hh in range(4):
                                    hs = slice(hh * 128, (hh + 1) * 128)
                                    P.mm(psf[:, hs], kt[slot][0:64, hh, tsl], qt[slot][0:64, hh, tsl],
                                         reads=[bkt[slot], bqt[slot]], writes=[bpsf])
                                    P.mm(psb[:, hs], kt[slot][64:128, hh, tsl], qt[slot][64:128, hh, tsl],
                                         reads=[bkt[slot], bqt[slot]], writes=[bpsb])
                                P.tt("dve", PT[t][:, :, 0, :], psf[:].rearrange("p (h c) -> p h c", h=4),
                                     cst[:, 768:896].unsqueeze(1).broadcast_to([128, 4, 128]), ALU.mult,
                                     reads=[bpsf, bcst], pwrites=[bPT[t]])
                                P.tt("dve", PT[t][:, :, 1, :], psb[:].rearrange("p (h c) -> p h c", h=4),
                                     cst[:, 896:1024].unsqueeze(1).broadcast_to([128, 4, 128]), ALU.mult,
                                     reads=[bpsb, bcst], pwrites=[bPT[t]])
                                pm, bpm = psum.next()
                                for g_ in range(4):
                                    gs = slice(g_ * 128, (g_ + 1) * 128)
                                    P.mm(pm[:, gs], vn[slot][t][:, g_, :], wst_b[:, g_, :], reads=[bvn[slot][t], bW],
                                         writes=[bpm])
                                for g_ in range(4):
                                    gs = slice(g_ * 128, (g_ + 1) * 128)
                                    P.stt("dve", sgt[:, g_, :], pm[:, gs],
                                          vecs[:, V_SGU + l * 4 + g_:V_SGU + l * 4 + g_ + 1], bs_bc[:, gs],
                                          ALU.mult, ALU.add, reads=[bpm, bconst, bW], pwrites=[bsgt])
                                P.tt("pool", mT[:, 4:8, tsl], sgt[:], gu[slot][:, :, tsl], ALU.mult,
                                     reads=[bsgt, bgu[slot]], pwrites=[bmT])
                                yield
                            for t in range(TPG):
                                n = gi * TPG + t
                                tsl = slice(t * 128, (t + 1) * 128)
                                po, bpo = psum.next(hold=True)
                                for hh in range(4):
                                    hs = slice(hh * 128, (hh + 1) * 128)
                                    P.mm(po[:, hs], vtok[slot][t][:, hs], PT[t][:, hh, 0, :], start=True, stop=False,
                                         reads=[bvtok[slot][t], bPT[t]], writes=[bpo])
                                    P.mm(po[:, hs], vtok[slot][t][:, hs], PT[t][:, hh, 1, :], start=False,
                                         stop=False, reads=[bvtok[slot][t], bPT[t]], writes=[bpo])
                                    P.mm(po[:, hs], SS[:, n, hs], qt[slot][:, hh, tsl], start=False, stop=True,
                                         reads=[bSSf[n], bSSb[n], bqt[slot]], writes=[bpo])
                                P.act(osq[:], po[:], AF.Square, reads=[bpo], writes=[bosq])
                                po_l.append((po, bpo))
                                yield
                                pn, bpn = psum.next()
                                P.mm(pn[:, :], ones_b[:], osq[:], reads=[bosq, bconst], writes=[bpn])
                                P.act(ro[:], pn[:], AF.Ln, reads=[bpn], writes=[bro], scale=1.0 / 128, bias=EPS)
                                P.act(ro[:], ro[:], AF.Exp, reads=[bro], writes=[bro], scale=-0.5)
                                P.tt("dve", om[:], po[:], ro[:], ALU.mult, reads=[bpo, bro], writes=[bom])
                                psum.release(bpo)
                                P.tt("pool", mT[:, 0:4, tsl], om[:].rearrange("p (h c) -> p h c", h=4),
                                     sg[slot][:, :, tsl], ALU.mult, reads=[bom, bsg[slot]], pwrites=[bmT])
                                yield
                            for pr in range(4):
                                pq, bpq = psum.next()
                                for j in range(2):
                                    dmc = 2 * pr + j
                                    for mc in range(KC):
                                        P.mm(pq[:, j * G:(j + 1) * G], wout[:, mc, dmc * 128:(dmc + 1) * 128],
                                             mT[:, mc, :], start=(mc == 0), stop=(mc == KC - 1),
                                             reads=[bmT, bW], writes=[bpq])
                                hf = pr // 2
                                P.tt("dve", xo[:, 2 * (pr % 2):2 * (pr % 2) + 2, :],
                                     pq[:].rearrange("p (b t) -> p b t", b=2), xg[slot][:, 2 * pr:2 * pr + 2, :],
                                     ALU.add, reads=[bpq, bxg[slot][pr]], pwrites=[bxo])
                                if pr % 2 == 1:
                                    t0_ = s * seq + gi * G
                                    P.dma(xs[hf * 512:(hf + 1) * 512, t0_:t0_ + G].rearrange("(kc p) t -> p kc t", p=128),
                                          xo[:], f"stx{hf}", reads=[bxo], writes=[bxs_h[hf]])
                                    if hf == 1:
                                        bxs[s][gi].w = bxs_h[1].w
                                        bxs[s][gi].r = {}
                                        bxs[s][gi].pw = dict([bxs_h[0].w])
                                yield
                            if g_next2 is not None:
                                load_x(s, g_next2, slot)

                        order = list(range(NG - 1, -1, -1))
                        load_x(s, order[0], 0)
                        if NG > 1:
                            load_x(s, order[1], 1)
                        fgs = {oi: front_gen(oi % 2, "p2", s, order[oi]) for oi in range(NG)}
                        for n in (order[0] * TPG + 1, order[0] * TPG):
                            bwd_step(n)
                        next(fgs[0])
                        drive(p2_A(order[0], 0, fgs[0]))
                        if NG > 1:
                            next(fgs[1])
                        for oi, gi in enumerate(order):
                            slot = oi % 2
                            nxt_g = order[oi + 1] if oi + 1 < NG else None
                            if nxt_g is not None:
                                for n in (nxt_g * TPG + 1, nxt_g * TPG):
                                    bwd_step(n)
                            ga = p2_A(nxt_g, 1 - slot, fgs[oi + 1]) if nxt_g is not None else None
                            gb = p2_B(gi, slot, order[oi + 2] if oi + 2 < NG else None)
                            drive_pattern("BBBABBABAAAAAABBBBAAAAAAA", ga, gb)
                            if oi + 2 < NG:
                                next(fgs[oi + 2])
                    P.barrier()
                    P.flush(sems)
                    if stop == 4:
                        return nc

            with contextlib.ExitStack() as ph:
                EP = ph.enter_context
                w1b = EP(sbt("w1b", [128, KC, DFF], BF16))
                w2b = EP(sbt("w2b", [128, FC, D], BF16))
                bW = Buf()
                with contextlib.ExitStack() as ld:
                    EL = ld.enter_context
                    NST = 3
                    stg = [EL(sbt(f"stgm{i}", [128, DFF], F32)) for i in range(NST)]
                    bstg = [Buf() for _ in range(NST)]
                    ceng = ("dve", "act", "dve", "act", "dve", "act", "pool", "dve")
                    n_ld = 0
                    for kc in range(KC):
                        sl = n_ld % NST
                        n_ld += 1
                        P.dma(stg[sl][:], w1[l * D + kc * 128:l * D + (kc + 1) * 128, :], f"ldw{sl}",
                              writes=[bstg[sl]])
                        gm = vecs[:, V_MLP + l * 8 + kc:V_MLP + l * 8 + kc + 1]
                        for qd in range(8):
                            eng = ceng[qd]
                            o_ = w1b[:, kc, qd * 512:(qd + 1) * 512]
                            i_ = stg[sl][:, qd * 512:(qd + 1) * 512]
                            if eng == "act":
                                P.act(o_, i_, AF.Copy, reads=[bstg[sl], bconst], writes=[bW], scale=gm)
                            else:
                                P.ts(eng, o_, i_, gm, ALU.mult, reads=[bstg[sl], bconst], pwrites=[bW])
                    for j in range(FC // 4):
                        sl = n_ld % NST
                        n_ld += 1
                        P.dma(stg[sl][:].rearrange("p (a n) -> p a n", a=4),
                              w2[l * DFF + j * 512:l * DFF + (j + 1) * 512, :].rearrange("(a p) n -> p a n", p=128),
                              f"ldw{sl}", writes=[bstg[sl]])
                        for qd in range(8):
                            eng = ceng[qd]
                            P.cp(eng, w2b[:, 4 * j + qd // 2, (qd % 2) * 512:(qd % 2 + 1) * 512],
                                 stg[sl][:, qd * 512:(qd + 1) * 512], reads=[bstg[sl]], pwrites=[bW])
                    P.barrier()
                    P.flush(sems)
                with contextlib.ExitStack() as cs:
                    EC = cs.enter_context
                    xg = [EC(sbt(f"xm{i}", [128, KC, G], F32)) for i in range(2)]
                    bxg = [[Buf() for _ in range(4)] for _ in range(2)]
                    h2 = [EC(sbt(f"h2_{i}", [128, KC, G], BF16)) for i in range(2)]
                    bh2 = [Buf(), Buf()]
                    sq = EC(sbt("sqm", [128, KC, G], BF16)); bsq = Buf()
                    sqs = EC(sbt("sqsm", [128, G], BF16)); bsqs = Buf()
                    lnt = EC(sbt("lntm", [128, G], F32)); blnt = Buf()
                    rstd = EC(sbt("rstdm", [128, G], F32)); brstd = Buf()
                    h1 = EC(sbt("h1", [128, FC, G], BF16))
                    bh1 = [Buf() for _ in range(FC // 2)]
                    rl = [EC(sbt(f"rl{i}", [128, 2 * G], F32)) for i in range(3)]
                    brl = [Buf() for _ in range(3)]

                    def _red(e):
                        with nc.allow_low_precision("8-term sum of squares feeding a bf16 matmul operand"):
                            return e.tensor_reduce(out=sqs[:], in_=sq[:].rearrange("p kc t -> p t kc"),
                                                   axis=AX.X, op=ALU.add)

                    def norm_early(slot):
                        P.act(sq[:], xg[slot][:], AF.Square, reads=bxg[slot], writes=[bsq])
                        if not last:
                            P.op("dve", _red, reads=[bsq], writes=[bsqs])

                    def norm_late(slot):
                        ps, bps = psum.next()
                        if last:
                            for kc in range(KC):
                                P.mm(ps[:, 0:G], ones_b[:], sq[:, kc, :], start=(kc == 0), stop=(kc == KC - 1),
                                     reads=[bsq, bconst], writes=[bps])
                        else:
                            P.mm(ps[:, 0:G], ones_b[:], sqs[:], reads=[bsqs, bconst], writes=[bps])
                        P.act(lnt[:], ps[:, 0:G], AF.Ln, reads=[bps], writes=[blnt], scale=1.0 / D, bias=EPS)
                        P.act(rstd[:], lnt[:], AF.Exp, reads=[blnt], writes=[brstd], scale=-0.5)

                    def norm_stats(slot):
                        norm_early(slot)
                        norm_late(slot)

                    def h2_mul(slot):
                        P.tt("dve", h2[slot][:], xg[slot][:], rstd[:].unsqueeze(1).broadcast_to([128, KC, G]),
                             ALU.mult, reads=bxg[slot] + [brstd], writes=[bh2[slot]])

                    def front_m(slot):
                        norm_stats(slot)
                        h2_mul(slot)

                    seq_groups = [(s, gi) for s in range(nseq) for gi in range(NG)]

                    def load_m(idx, slot):
                        s, gi = seq_groups[idx]
                        P.dma(xg[slot][:], xview(xs, s, gi), f"ldx{slot}", reads=[bxs[s][gi]], writes=bxg[slot])

                    def finish_final(pslot, ps_, pgi):
                        norm_late(pslot)
                        for kc in range(KC):
                            P.stt("dve", xg[pslot][:, kc, :], xg[pslot][:, kc, :],
                                  vecs[:, V_FIN + kc:V_FIN + kc + 1], rstd[:], ALU.mult, ALU.mult,
                                  reads=[bxg[pslot][kc // 2], brstd, bconst], writes=[bxg[pslot][kc // 2]])
                        P.dma(xview(yT, ps_, pgi), xg[pslot][:], f"stx{pslot}", reads=bxg[pslot],
                              writes=[bxs[ps_][pgi]])

                    pending = None
                    load_m(0, 0)
                    front_m(0)
                    nrl = 0
                    for idx, (s, gi) in enumerate(seq_groups):
                        slot = idx % 2
                        if not last and idx + 1 < len(seq_groups):
                            load_m(idx + 1, 1 - slot)
                        for pr in range(FC // 2):
                            pf, bpf = psum.next()
                            for j in range(2):
                                fc = 2 * pr + j
                                for kc in range(KC):
                                    P.mm(pf[:, j * G:(j + 1) * G], w1b[:, kc, fc * 128:(fc + 1) * 128],
                                         h2[slot][:, kc, :], start=(kc == 0), stop=(kc == KC - 1),
                                         reads=[bh2[slot], bW], writes=[bpf])
                            r_ = nrl % 3
                            nrl += 1
                            if pr % 2 == 0:
                                P.act(rl[r_][:], pf[:], AF.Relu, reads=[bpf], writes=[brl[r_]])
                            else:
                                P.ts("dve", rl[r_][:], pf[:], 0.0, ALU.max, reads=[bpf], writes=[brl[r_]])
                            P.tt("pool", h1[:, 2 * pr:2 * pr + 2, :], rl[r_][:].rearrange("p (b t) -> p b t", b=2),
                                 rl[r_][:].rearrange("p (b t) -> p b t", b=2), ALU.mult, reads=[brl[r_]],
                                 writes=[bh1[pr]])
                            if last and pr == 5:
                                if pending is not None:
                                    finish_final(*pending)
                                    pending = None
                                if idx + 1 < len(seq_groups):
                                    load_m(idx + 1, 1 - slot)
                            if idx + 1 < len(seq_groups):
                                if pr == (12 if last else 7):
                                    norm_early(1 - slot)
                                elif pr == (15 if last else 12):
                                    norm_late(1 - slot)
                                    h2_mul(1 - slot)
                        for pr in range(4):
                            pq, bpq = psum.next()
                            for j in range(2):
                                dmc = 2 * pr + j
                                for fc in range(FC):
                                    P.mm(pq[:, j * G:(j + 1) * G], w2b[:, fc, dmc * 128:(dmc + 1) * 128], h1[:, fc, :],
                                         start=(fc == 0), stop=(fc == FC - 1), reads=[bh1[fc // 2], bW],
                                         writes=[bpq])
                            P.tt("dve", xg[slot][:, 2 * pr:2 * pr + 2, :], pq[:].rearrange("p (b t) -> p b t", b=2),
                                 xg[slot][:, 2 * pr:2 * pr + 2, :], ALU.add, reads=[bpq, bxg[slot][pr]],
                                 writes=[bxg[slot][pr]])
                        if not last:
                            P.dma(xview(xs, s, gi), xg[slot][:], f"stx{slot}", reads=bxg[slot],
                                  writes=[bxs[s][gi]])
                        else:
                            norm_early(slot)
                            pending = (slot, s, gi)
                    if pending is not None:
                        finish_final(*pending)
                    P.barrier()
                    P.flush(sems)
    return nc


def _consts():
    j = np.arange(128)[:, None]
    c = np.arange(128)[None, :]
    s = np.float32(-1.0 / 16.0)
    cst = np.zeros((128, 1025), np.float32)
    cst[:, 0:128] = (j > c) * s
    cst[:, 128:256] = (j < c) * s
    cst[:, 256:384] = (j <= c) * s
    cst[:, 384:512] = (j >= c) * s
    cst[:, 512:640] = 1.0
    cst[:, 768:896] = (c >= j)
    cst[:, 896:1024] = (j > c)
    cst[:, 1024] = s
    return cst


def make_shared(depth, norm_mix_g, w_in, w_a2_fwd, b_a_fwd, w_a2_bwd, b_a_bwd, gla_norm_g, sgu_norm_g, w_s, b_s,
                w_out, norm_mlp_g, w_mlp1, w_mlp2, final_norm_g):
    f = lambda a: np.ascontiguousarray(np.asarray(a, dtype=np.float32))
    L = depth
    vecs = np.concatenate([
        f(norm_mix_g).reshape(L, 8, 128).transpose(2, 0, 1).reshape(128, 8 * L),
        f(norm_mlp_g).reshape(L, 8, 128).transpose(2, 0, 1).reshape(128, 8 * L),
        f(gla_norm_g).reshape(L, 4, 128).transpose(2, 0, 1).reshape(128, 4 * L),
        f(sgu_norm_g).reshape(L, 4, 128).transpose(2, 0, 1).reshape(128, 4 * L),
        f(final_norm_g).reshape(8, 128).T,
    ], axis=1)
    return {
        "w_in": f(w_in).reshape(L * D, DIN),
        "w_out": f(w_out).reshape(L * D, D),
        "w1": f(w_mlp1).reshape(L * D, DFF),
        "w2": f(w_mlp2).reshape(L * DFF, D),
        "wa2f": f(w_a2_fwd).reshape(L * 16, 256),
        "wa2b": f(w_a2_bwd).reshape(L * 16, 256),
        "baf": f(b_a_fwd).reshape(L, 256),
        "bab": f(b_a_bwd).reshape(L, 256),
        "wsT": f(np.asarray(w_s, np.float32).transpose(0, 1, 3, 2)).reshape(L * 4 * 128, 128),
        "bs": f(b_s).reshape(L, 512),
        "vecs": f(vecs),
        "cst": _consts(),
    }


def run(x, shared, depth, n_cores, nseq, stop=99):
    x = np.asarray(x, dtype=np.float32)
    B, S, _ = x.shape
    assert B == n_cores * nseq
    nc = build_program(depth, nseq, S, stop)
    in_maps = []
    for c in range(n_cores):
        xT = np.ascontiguousarray(x[c * nseq:(c + 1) * nseq].reshape(nseq * S, D).T)
        m = dict(shared)
        m["xT"] = xT
        in_maps.append(m)
    res = run_bass_kernel_spmd(nc, in_maps, core_ids=list(range(n_cores)))
    out = np.empty((B, S, D), np.float32)
    for c in range(n_cores):
        out[c * nseq:(c + 1) * nseq] = res.results[c]["yT"].T.reshape(nseq, S, D)
    return out


def kernel(x, norm_mix_g, w_in, w_a2_fwd, b_a_fwd, w_a2_bwd, b_a_bwd, gla_norm_g, sgu_norm_g, w_s, b_s, w_out,
           norm_mlp_g, w_mlp1, w_mlp2, final_norm_g):
    depth = 4
    shared = make_shared(depth, norm_mix_g, w_in, w_a2_fwd, b_a_fwd, w_a2_bwd, b_a_bwd, gla_norm_g, sgu_norm_g,
                         w_s, b_s, w_out, norm_mlp_g, w_mlp1, w_mlp2, final_norm_g)
    return run(x, shared, depth, 8, 2)
```

```python
import contextlib
import numpy as np
import concourse.bass as bass
import concourse.mybir as mybir
from concourse.bass_utils import run_bass_kernel_spmd

F32 = mybir.dt.float32
BF16 = mybir.dt.bfloat16
AF = mybir.ActivationFunctionType
ALU = mybir.AluOpType
AX = mybir.AxisListType

D = 1024
KC = 8
DIN = 2592
DFF = 4096
FC = 32
G = 256
TPG = 2
EPS = 1e-6
NWF = 16 * 128 + 32


class Buf:
    __slots__ = ("w", "r", "pw")

    def __init__(self):
        self.w = None
        self.r = {}
        self.pw = {}


class Prog:
    CE = ("pe", "act", "dve", "pool")

    def __init__(self, nc):
        self.nc = nc
        self.q = {e: [] for e in ("pe", "act", "dve", "pool", "sp")}
        self.n = {e: 0 for e in self.CE}
        self.waited = {e: {} for e in self.q}
        self.dcnt = {}
        self.ninst = 0

    def _wait(self, eng, key, val):
        if key == eng and eng == "pe":
            return
        if self.waited[eng].get(key, 0) >= val:
            return
        self.waited[eng][key] = val
        self.q[eng].append(("w", key, val))

    def _deps(self, eng, reads, writes, pwrites=()):
        for b in reads:
            if b.w is not None:
                self._wait(eng, *b.w)
            for k, v in b.pw.items():
                self._wait(eng, k, v)
        for b in writes:
            if b.w is not None:
                self._wait(eng, *b.w)
            for k, v in b.pw.items():
                self._wait(eng, k, v)
            for k, v in b.r.items():
                self._wait(eng, k, v)
        for b in pwrites:
            if b.w is not None:
                self._wait(eng, *b.w)
            for k, v in b.r.items():
                self._wait(eng, k, v)

    @staticmethod
    def _mark(tok, reads, writes, pwrites=()):
        k, v = tok
        for b in pwrites:
            if b.r:
                b.r = {}
                b.pw = {}
                b.w = None
            if b.pw.get(k, 0) < v:
                b.pw[k] = v
        for b in reads:
            if b.r.get(k, 0) < v:
                b.r[k] = v
        for b in writes:
            b.w = tok
            b.r = {}
            b.pw = {}

    def op(self, eng, fn, reads=(), writes=(), pwrites=()):
        self._deps(eng, reads, writes, pwrites)
        self.n[eng] += 1
        tok = (eng, self.n[eng])
        self.q[eng].append(("o", fn))
        self._mark(tok, reads, writes, pwrites)
        self.ninst += 1
        return tok

    def dma(self, out_ap, in_ap, semkey, reads=(), writes=(), eng="sp", pwrites=()):
        self._deps(eng, reads, writes, pwrites)
        c = self.dcnt.get(semkey, 0) + 16
        self.dcnt[semkey] = c
        tok = (semkey, c)
        self.q[eng].append(("d", out_ap, in_ap, semkey))
        self._mark(tok, reads, writes, pwrites)
        self.ninst += 1
        return tok

    def barrier(self):
        for e in self.q:
            for k, v in self.n.items():
                if v:
                    self._wait(e, k, v) if k != e else None
            for k, v in self.dcnt.items():
                self._wait(e, k, v)

    def flush(self, sems):
        nc = self.nc
        q = self.q

        def run(engine, items, ekey):
            for it in items:
                if it[0] == "w":
                    engine.wait_ge(sems[it[1]], it[2])
                elif it[0] == "o":
                    it[1](engine).then_inc(sems[ekey], 1)
                else:
                    engine.dma_start(out=it[1], in_=it[2]).then_inc(sems[it[3]], 16)

        with nc.Block() as block:
            @block.tensor
            def _(e):
                run(e, q["pe"], "pe")

            @block.scalar
            def _(e):
                run(e, q["act"], "act")

            @block.vector
            def _(e):
                run(e, q["dve"], "dve")

            @block.gpsimd
            def _(e):
                run(e, q["pool"], "pool")

            @block.sync
            def _(e):
                run(e, q["sp"], "sp")
        for k in q:
            q[k] = []

    def mm(self, out, lhsT, rhs, start=True, stop=True, reads=(), writes=()):
        return self.op("pe", lambda e: e.matmul(out, lhsT=lhsT, rhs=rhs, start=start, stop=stop), reads, writes)

    def act(self, out, in_, func, reads=(), writes=(), scale=1.0, bias=0.0, pwrites=()):
        return self.op("act", lambda e: e.activation(out=out, in_=in_, func=func, bias=bias, scale=scale),
                       reads, writes, pwrites)

    def tt(self, eng, out, in0, in1, op, reads=(), writes=(), pwrites=()):
        return self.op(eng, lambda e: e.tensor_tensor(out=out, in0=in0, in1=in1, op=op), reads, writes, pwrites)

    def ts(self, eng, out, in0, s1, op0, reads=(), writes=(), s2=None, op1=None, pwrites=()):
        if op1 is None:
            return self.op(eng, lambda e: e.tensor_scalar(out=out, in0=in0, scalar1=s1, scalar2=None, op0=op0),
                           reads, writes, pwrites)
        return self.op(eng, lambda e: e.tensor_scalar(out=out, in0=in0, scalar1=s1, scalar2=s2, op0=op0, op1=op1),
                       reads, writes, pwrites)

    def stt(self, eng, out, in0, scalar, in1, op0, op1, reads=(), writes=(), pwrites=()):
        return self.op(eng, lambda e: e.scalar_tensor_tensor(out=out, in0=in0, scalar=scalar, in1=in1,
                                                             op0=op0, op1=op1), reads, writes, pwrites)

    def cp(self, eng, out, in_, reads=(), writes=(), pwrites=()):
        if eng == "act":
            return self.act(out, in_, AF.Copy, reads, writes, pwrites=pwrites)
        return self.op(eng, lambda e: e.tensor_copy(out=out, in_=in_), reads, writes, pwrites)

    def memset(self, eng, ap, val, writes=()):
        return self.op(eng, lambda e: e.memset(ap, val), (), writes)


class PsumRing:
    def __init__(self, banks):
        self.banks = banks
        self.bufs = [Buf() for _ in banks]
        self.held = set()
        self.i = 0

    def next(self, hold=False):
        for _ in range(len(self.banks)):
            i = self.i
            self.i = (i + 1) % len(self.banks)
            if i not in self.held:
                if hold:
                    self.held.add(i)
                return self.banks[i], self.bufs[i]
        raise RuntimeError("all PSUM banks held")

    def release(self, buf):
        self.held.discard(self.bufs.index(buf))


def build_program(depth, nseq, seq, stop=99):
    NT = nseq * seq
    NG = seq // G
    NCH = seq // 128
    nc = bass.Bass("TRN2", target_bir_lowering=False)
    _uid = [0]

    def sbt(name, shape, dt):
        _uid[0] += 1
        return nc.sbuf_tensor(f"{name}_u{_uid[0]}", shape, dt)

    def din(name, shape):
        return nc.dram_tensor(name, list(shape), F32, kind="ExternalInput").ap()

    xT = din("xT", [D, NT])
    w_in = din("w_in", [depth * D, DIN])
    w_out = din("w_out", [depth * D, D])
    w1 = din("w1", [depth * D, DFF])
    w2 = din("w2", [depth * DFF, D])
    wa2f = din("wa2f", [depth * 16, 256])
    wa2b = din("wa2b", [depth * 16, 256])
    baf = din("baf", [depth, 256])
    bab = din("bab", [depth, 256])
    wsT = din("wsT", [depth * 4 * 128, 128])
    bsd = din("bs", [depth, 512])
    vecs_d = din("vecs", [128, 24 * depth + 8])
    cst_d = din("cst", [128, 1025])
    yT = nc.dram_tensor("yT", [D, NT], F32, kind="ExternalOutput").ap()
    xs = yT
    vscr = nc.dram_tensor("vscr", [nseq * NCH, 128, 512], BF16).ap()
    lscr = nc.dram_tensor("lscr", [nseq * NCH, 128, 512], BF16).ap()

    V_MIX = 0
    V_MLP = 8 * depth
    V_GLA = 16 * depth
    V_SGU = 20 * depth
    V_FIN = 24 * depth

    with contextlib.ExitStack() as top:
        E = top.enter_context
        sem_names = ["pe", "act", "dve", "pool", "ldx0", "ldx1", "stx0", "stx1", "ldw0", "ldw1", "ldw2", "ldc", "ldk",
                     "sl00", "sl01", "sl10", "sl11", "sv0", "sv1", "ll00", "ll01", "ll10", "ll11",
                     "lv00", "lv01", "lv10", "lv11"]
        sems = {k: E(nc.semaphore(k)) for k in sem_names}
        P = Prog(nc)
        psum = PsumRing([E(nc.psum_tensor(f"ps{i}", [128, 512], F32)) for i in range(8)])

        cst = E(sbt("cst", [128, 1025], F32))
        vecs = E(sbt("vecs", [128, 24 * depth + 8], F32))
        ugt_b = E(sbt("ugt_b", [128, 128], BF16))
        ult_b = E(sbt("ult_b", [128, 128], BF16))
        ucat_b = E(sbt("ucat_b", [128, 256], BF16))
        ones_b = E(sbt("ones_b", [128, 128], BF16))
        negc_b = E(sbt("negc_b", [128, 2], BF16))
        bconst = Buf()
        bcst = Buf()
        P.dma(cst[:], cst_d[:, :], "ldk", writes=[bcst])
        t = P.dma(vecs[:], vecs_d[:, :], "ldk", writes=[bconst])
        bcst.w = t
        maskcat = cst[:, 768:1024]
        P.cp("dve", ugt_b[:], cst[:, 0:128], reads=[bcst], writes=[bconst])
        P.cp("dve", ult_b[:], cst[:, 128:256], reads=[bcst], writes=[bconst])
        P.cp("dve", ucat_b[:], cst[:, 256:512], reads=[bcst], writes=[bconst])
        P.cp("dve", ones_b[:], cst[:, 512:640], reads=[bcst], writes=[bconst])
        P.cp("dve", negc_b[:, 0:1], cst[:, 1024:1025], reads=[bcst], writes=[bconst])

        bxs = [[Buf() for _ in range(NG)] for _ in range(nseq)]
        bvs = [[Buf() for _ in range(NCH)] for _ in range(nseq)]
        bls = [[Buf() for _ in range(NCH)] for _ in range(nseq)]

        def xview(t_ap, s, gi):
            t0 = s * seq + gi * G
            return t_ap[:, t0:t0 + G].rearrange("(kc p) t -> p kc t", p=128)

        for l in range(depth):
            last = (l == depth - 1)
            with contextlib.ExitStack() as ph:
                EP = ph.enter_context
                winf = EP(sbt("winf", [128, KC, NWF], BF16))
                wint = EP(sbt("wint", [128, KC, 1024], BF16))
                wink = EP(sbt("wink", [128, KC, 256], BF16))
                wout = EP(sbt("wout", [128, KC, D], BF16))
                waug_b = EP(sbt("waug_b", [33, 512], BF16))
                wst_b = EP(sbt("wst_b", [128, 4, 128], BF16))
                bs_bc = EP(sbt("bs_bc", [128, 512], F32))
                SS = EP(sbt("SS", [128, NCH, 512], BF16))
                dec = EP(sbt("dec", [128, NCH, 4], F32))
                bW = Buf()
                with contextlib.ExitStack() as ld:
                    EL = ld.enter_context
                    stg = [EL(sbt(f"stg{i}", [128, DIN], F32)) for i in range(2)]
                    bstg = [Buf(), Buf()]
                    waug_f = EL(sbt("waug_f", [33, 512], F32))
                    wst_f = EL(sbt("wst_f", [128, 4, 128], F32))
                    bsm = Buf()
                    P.memset("pool", waug_f[:], 0.0, writes=[bsm])
                    wv = waug_f[:].rearrange("p (h d k) -> p h d k", h=4, d=2)
                    P.dma(wv[0:16, :, 0, :], wa2f[l * 16:(l + 1) * 16, :].rearrange("p (h k) -> p h k", h=4),
                          "ldc", writes=[bsm])
                    P.dma(wv[16:32, :, 1, :], wa2b[l * 16:(l + 1) * 16, :].rearrange("p (h k) -> p h k", h=4),
                          "ldc", writes=[bsm])
                    P.dma(wv[32:33, :, 0, :], baf[l:l + 1, :].rearrange("p (h k) -> p h k", h=4), "ldc", writes=[bsm])
                    P.dma(wv[32:33, :, 1, :], bab[l:l + 1, :].rearrange("p (h k) -> p h k", h=4), "ldc", writes=[bsm])
                    P.dma(wst_f[:], wsT[l * 512:(l + 1) * 512, :].rearrange("(g q) p -> q g p", g=4), "ldc",
                          writes=[bsm])
                    t = P.dma(bs_bc[:], bsd[l:l + 1, :].partition_broadcast(128), "ldc", writes=[bsm], pwrites=[bW])
                    bsm.w = t
                    P.cp("dve", waug_b[:], waug_f[:], reads=[bsm], pwrites=[bW])
                    P.cp("pool", wst_b[:], wst_f[:], reads=[bsm], pwrites=[bW])
                    rr = 0
                    engs = ("dve", "pool")
                    for kc in range(KC):
                        sl = kc % 2
                        P.dma(stg[sl][:], w_in[l * D + kc * 128:l * D + (kc + 1) * 128, :], f"ldw{sl}",
                              writes=[bstg[sl]])
                        gm = vecs[:, V_MIX + l * 8 + kc:V_MIX + l * 8 + kc + 1]
                        s_ = stg[sl]
                        fv = winf[:, kc, 0:1024].rearrange("p (b d k) -> p b d k", b=8, d=2)
                        jobs = []
                        for d_ in range(2):
                            jobs.append((fv[:, 0:4, d_, :], s_[:, 0:256].rearrange("p (h k) -> p h k", h=4), 0.125))
                            jobs.append((fv[:, 4:8, d_, :], s_[:, 256:512].rearrange("p (h k) -> p h k", h=4), None))
                        jobs.append((wink[:, kc, :], s_[:, 256:512], None))
                        jobs.append((wint[:, kc, 0:512], s_[:, 512:1024], None))
                        jobs.append((winf[:, kc, 1024:1536], s_[:, 1024:1536], None))
                        jobs.append((winf[:, kc, 2048:2080], s_[:, 1536:1568], None))
                        jobs.append((winf[:, kc, 1536:2048], s_[:, 1568:2080], None))
                        jobs.append((wint[:, kc, 512:1024], s_[:, 2080:2592], None))
                        jeng = ("dve", "act", "dve", "pool", "act", "dve", "act", "pool", "dve", "act")
                        for ji, (o_, i_, sc) in enumerate(jobs):
                            eng = jeng[ji]
                            if sc is not None:
                                P.ts("dve", o_, i_, gm, ALU.mult, reads=[bstg[sl], bconst], pwrites=[bW],
                                     s2=sc, op1=ALU.mult)
                            elif eng == "act":
                                P.act(o_, i_, AF.Copy, reads=[bstg[sl], bconst], writes=[bW], scale=gm)
                            else:
                                P.ts(eng, o_, i_, gm, ALU.mult, reads=[bstg[sl], bconst], pwrites=[bW])
                    for kc in range(KC):
                        sl = kc % 2
                        P.dma(stg[sl][:, 0:D], w_out[l * D + kc * 128:l * D + (kc + 1) * 128, :], f"ldw{sl}",
                              writes=[bstg[sl]])
                        for hf in range(2):
                            eng = ("dve", "act")[hf]
                            o_ = wout[:, kc, hf * 512:(hf + 1) * 512]
                            i_ = stg[sl][:, hf * 512:(hf + 1) * 512]
                            if kc < 4:
                                gg = vecs[:, V_GLA + l * 4 + kc:V_GLA + l * 4 + kc + 1]
                                if eng == "act":
                                    P.act(o_, i_, AF.Copy, reads=[bstg[sl], bconst], writes=[bW], scale=gg)
                                else:
                                    P.ts(eng, o_, i_, gg, ALU.mult, reads=[bstg[sl], bconst], pwrites=[bW])
                            else:
                                P.cp(eng, o_, i_, reads=[bstg[sl]], pwrites=[bW])
                    P.barrier()
                    P.flush(sems)
                    if stop == 1:
                        return nc

                with contextlib.ExitStack() as cs:
                    EC = cs.enter_context
                    xg = [EC(sbt(f"xg{i}", [128, KC, G], F32)) for i in range(2)]
                    bxg = [[Buf() for _ in range(4)] for _ in range(2)]
                    hb = [EC(sbt(f"hb{i}", [128, KC, G], BF16)) for i in range(2)]
                    bh = [[Buf(), Buf()] for _ in range(2)]
                    sq = EC(sbt("sq", [128, KC, G], BF16)); bsq = Buf()
                    sqs = EC(sbt("sqs", [128, G], BF16)); bsqs = Buf()
                    lnt = EC(sbt("lnt", [128, G], F32)); blnt = Buf()
                    rstd = EC(sbt("rstd", [128, G], F32)); brstd = Buf()
                    aaug = [EC(sbt(f"aaug{i}", [33, G], BF16)) for i in range(2)]
                    baaug = [Buf(), Buf()]
                    esb = EC(sbt("esb", [128, 512], F32)); besb = Buf()
                    lsb = [[EC(sbt(f"lsb{i}_{t}", [128, 512], BF16)) for t in range(TPG)] for i in range(2)]
                    blsb = [[Buf() for _ in range(TPG)] for _ in range(2)]
                    edsb = [EC(sbt(f"edsb{i}", [128, 512], BF16)) for i in range(TPG)]; bedsb = [Buf(), Buf()]
                    khat = [EC(sbt(f"khat{i}", [128, 512], BF16)) for i in range(TPG)]; bkhat = [Buf(), Buf()]
                    vtok = [[EC(sbt(f"vtok{i}_{t}", [128, 512], BF16)) for t in range(TPG)] for i in range(2)]
                    bvtok = [[Buf() for _ in range(TPG)] for _ in range(2)]
                    Sst = [EC(sbt(f"Sst{i}", [128, 512], F32)) for i in range(2)]
                    Eb = EC(sbt("Eb", [128, 4, G], BF16)); bEb = Buf()
                    Einv = EC(sbt("Einv", [128, 4, G], BF16)); bEinv = Buf()
                    qt = [EC(sbt(f"qt{i}", [128, 4, G], BF16)) for i in range(2)]; bqt = [Buf(), Buf()]
                    kt = [EC(sbt(f"kt{i}", [128, 4, G], BF16)) for i in range(2)]; bkt = [Buf(), Buf()]
                    sg = [EC(sbt(f"sg{i}", [128, 4, G], BF16)) for i in range(2)]; bsg = [Buf(), Buf()]
                    gu = [EC(sbt(f"gu{i}", [128, 4, G], BF16)) for i in range(2)]; bgu = [Buf(), Buf()]
                    gsvt = [EC(sbt(f"gsv{i}", [128, 512], F32)) for i in range(TPG)]; bgsvt = [Buf(), Buf()]
                    ss4b = EC(sbt("ss4b", [128, 8], F32)); bss4b = Buf()
                    sqv = esb; bsqv = besb
                    ss4 = EC(sbt("ss4", [128, 8], F32)); bss4 = Buf()
                    vn = [[EC(sbt(f"vn{i}_{t}", [128, 4, 128], BF16)) for t in range(TPG)] for i in range(2)]
                    bvn = [[Buf() for _ in range(TPG)] for _ in range(2)]
                    PT = [EC(sbt(f"PT{i}", [128, 4, 2, 128], BF16)) for i in range(TPG)]
                    bPT = [Buf() for _ in range(TPG)]
                    osq = EC(sbt("osq", [128, 512], BF16)); bosq = Buf()
                    ro = EC(sbt("ro", [128, 512], F32)); bro = Buf()
                    om = EC(sbt("om", [128, 512], F32)); bom = Buf()
                    sgt = EC(sbt("sgt", [128, 4, 128], F32)); bsgt = Buf()
                    mT = EC(sbt("mT", [128, KC, G], BF16)); bmT = Buf()
                    xo = EC(sbt("xo", [128, 4, G], F32)); bxo = Buf()
                    bxs_h = [Buf(), Buf()]

                    for i in range(2):
                        P.memset("pool", aaug[i][32:33, :], 1.0, writes=[baaug[i]])

                    def load_x(s, gi, slot):
                        if l == 0:
                            P.dma(xg[slot][:], xview(xT, s, gi), f"ldx{slot}", writes=bxg[slot])
                        else:
                            P.dma(xg[slot][:], xview(xs, s, gi), f"ldx{slot}", reads=[bxs[s][gi]],
                                  writes=bxg[slot])

                    def front_gen(slot, mode, s_, gi_):
                        P.act(sq[:], xg[slot][:], AF.Square, reads=bxg[slot], writes=[bsq])
                        yield
                        ps, bps = psum.next()
                        for kc in range(KC):
                            P.mm(ps[:, 0:G], ones_b[:], sq[:, kc, :], start=(kc == 0), stop=(kc == KC - 1),
                                 reads=[bsq, bconst], writes=[bps])
                        P.act(lnt[:], ps[:, 0:G], AF.Ln, reads=[bps], writes=[blnt], scale=1.0 / D, bias=EPS)
                        P.act(rstd[:], lnt[:], AF.Exp, reads=[blnt], writes=[brstd], scale=-0.5)
                        for hf, eng, k0_, k1_ in ((0, "dve", 0, 6), (1, "pool", 6, 8)):
                            P.tt(eng, hb[slot][:, k0_:k1_, :], xg[slot][:, k0_:k1_, :],
                                 rstd[:].unsqueeze(1).broadcast_to([128, k1_ - k0_, G]), ALU.mult,
                                 reads=bxg[slot] + [brstd], writes=[bh[slot][hf]])
                        yield
                        if mode == "p2":
                            for t in range(TPG):
                                n_ = s_ * NCH + gi_ * TPG + t
                                P.dma(lsb[slot][t][:], lscr[n_], f"ll{slot}{t}", reads=[bls[s_][gi_ * TPG + t]],
                                      writes=[blsb[slot][t]])
                                P.dma(vtok[slot][t][:], vscr[n_], f"lv{slot}{t}", reads=[bvs[s_][gi_ * TPG + t]],
                                      writes=[bvtok[slot][t]])
                            yield
                            yield
                            return
                        pa, bpa = psum.next()
                        for kc in range(KC):
                            P.mm(pa[0:32, 0:G], winf[:, kc, 2048:2080], hb[slot][:, kc, :], start=(kc == 0),
                                 stop=(kc == KC - 1), reads=[bh[slot][kc // 6], bW], writes=[bpa])
                        P.act(aaug[slot][0:32, :], pa[0:32, 0:G], AF.Copy, reads=[bpa], writes=[baaug[slot]])
                        yield
                        for t in range(TPG):
                            pl, bpl = psum.next()
                            P.mm(pl[:, :], aaug[slot][0:33, t * 128:(t + 1) * 128], waug_b[0:33, :],
                                 reads=[baaug[slot], bW], writes=[bpl])
                            P.act(esb[:], pl[:], AF.Exp, reads=[bpl], writes=[besb], scale=-1.0)
                            P.act(lsb[slot][t][:], esb[:], AF.Ln, reads=[besb], writes=[blsb[slot][t]], bias=1.0)
                            n_ = s_ * NCH + gi_ * TPG + t
                            P.dma(lscr[n_], lsb[slot][t][:], f"sl{slot}{t}", reads=[blsb[slot][t]],
                                  writes=[bls[s_][gi_ * TPG + t]])
                        yield

                    def drive_pattern(pattern, ga, gb):
                        for ch in pattern:
                            g = ga if ch == "A" else gb
                            if g is not None:
                                next(g, None)
                        drive(ga, gb)

                    def drive(*gens):
                        gens = [g for g in gens if g is not None]
                        while gens:
                            for g in list(gens):
                                try:
                                    next(g)
                                except StopIteration:
                                    gens.remove(g)

                    for s in range(nseq):
                        bSSf = [Buf() for _ in range(NCH)]
                        bSSb = [Buf() for _ in range(NCH)]
                        bdec = [Buf() for _ in range(NCH)]
                        bS = [[[Buf() for _ in range(4)] for _ in range(2)] for _ in range(2)]
                        P.memset("dve", Sst[0][0:64, :], 0.0, writes=bS[0][0])
                        P.memset("pool", Sst[0][64:128, :], 0.0, writes=bS[1][0])

                        def fwd_step(n):
                            cur, nxt = Sst[n % 2], Sst[(n + 1) % 2]
                            for hh in range(4):
                                hs = slice(hh * 128, (hh + 1) * 128)
                                P.stt("dve", nxt[0:64, hs], cur[0:64, hs], dec[0:64, n, hh:hh + 1], SS[0:64, n, hs],
                                      ALU.mult, ALU.add, reads=[bS[0][n % 2][hh], bdec[n], bSSf[n]],
                                      writes=[bS[0][(n + 1) % 2][hh]])
                            P.cp("pool", SS[0:64, n, :], cur[0:64, :], reads=bS[0][n % 2], writes=[bSSf[n]])

                        def bwd_step(n):
                            i = NCH - 1 - n
                            cur, nxt = Sst[i % 2], Sst[(i + 1) % 2]
                            P.tt("pool", nxt[64:128, :].rearrange("p (h v) -> p h v", h=4),
                                 cur[64:128, :].rearrange("p (h v) -> p h v", h=4),
                                 dec[64:128, n, :].unsqueeze(2).broadcast_to([64, 4, 128]), ALU.mult,
                                 reads=bS[1][i % 2] + [bdec[n]], writes=bS[1][(i + 1) % 2])
                            P.tt("pool", nxt[64:128, :], nxt[64:128, :], SS[64:128, n, :], ALU.add,
                                 reads=bS[1][(i + 1) % 2] + [bSSb[n]], writes=bS[1][(i + 1) % 2])
                            P.cp("pool", SS[64:128, n, :], cur[64:128, :], reads=bS[1][i % 2], writes=[bSSb[n]])

                        def p1_tiles(gi, slot):
                            banks = []
                            for t in range(TPG):
                                tsl = slice(t * 128, (t + 1) * 128)
                                pk, bpk = psum.next(hold=True)
                                pv, bpv = psum.next(hold=True)
                                for kc in range(KC):
                                    P.mm(pk[:, 0:256], hb[slot][:, kc, tsl], wink[:, kc, :], start=(kc == 0),
                                         stop=(kc == KC - 1), reads=[bh[slot][kc // 6], bW], writes=[bpk])
                                    P.mm(pv[:, :], hb[slot][:, kc, tsl], wint[:, kc, 0:512], start=(kc == 0),
                                         stop=(kc == KC - 1), reads=[bh[slot][kc // 6], bW], writes=[bpv])
                                P.cp("act", vtok[0][t][:], pv[:], reads=[bpv], writes=[bvtok[0][t]])
                                psum.release(bpv)
                                P.dma(vscr[s * NCH + gi * TPG + t], vtok[0][t][:], f"sv{t}", reads=[bvtok[0][t]],
                                      writes=[bvs[s][gi * TPG + t]])
                                banks.append((pk, bpk, pv, bpv))
                                yield
                            for t in range(TPG):
                                n = gi * TPG + t
                                pk, bpk, pv, bpv = banks[t]
                                pd, bpd = psum.next()
                                l4 = lsb[slot][t][:].rearrange("p (h d k) -> p h d k", h=4, d=2)
                                pd4 = pd[:].rearrange("p (h d k) -> p h d k", h=4, d=2)
                                P.mm(pd4[:, :, 0, :], ugt_b[:], l4[:, :, 0, :], reads=[blsb[slot][t], bconst],
                                     writes=[bpd])
                                P.mm(pd4[:, :, 1, :], ult_b[:], l4[:, :, 1, :], reads=[blsb[slot][t], bconst],
                                     writes=[bpd])
                                pdc, bpdc = psum.next()
                                for hh in range(4):
                                    hs = slice(hh * 128, (hh + 1) * 128)
                                    P.mm(pdc[:, hh:hh + 1], lsb[slot][t][:, hs], negc_b[:, 0:1],
                                         reads=[blsb[slot][t], bconst], writes=[bpdc])
                                P.act(edsb[t][:], pd[:], AF.Exp, reads=[bpd], writes=[bedsb[t]])
                                P.act(dec[:, n, :], pdc[:, 0:4], AF.Exp, reads=[bpdc], writes=[bdec[n]])
                                k4 = pk[:, 0:256].rearrange("p (h k) -> p h k", h=4).unsqueeze(2).broadcast_to(
                                    [128, 4, 2, 64])
                                P.tt("dve", khat[t][:].rearrange("p (h d k) -> p h d k", h=4, d=2),
                                     edsb[t][:].rearrange("p (h d k) -> p h d k", h=4, d=2), k4, ALU.mult,
                                     reads=[bpk, bedsb[t]], writes=[bkhat[t]])
                                psum.release(bpk)
                                yield
                            for t in range(TPG):
                                n = gi * TPG + t
                                pkv, bpkv = psum.next()
                                for hh in range(4):
                                    hs = slice(hh * 128, (hh + 1) * 128)
                                    P.mm(pkv[:, hs], khat[t][:, hs], vtok[0][t][:, hs],
                                         reads=[bkhat[t], bvtok[0][t]], writes=[bpkv])
                                P.cp("dve", SS[:, n, :], pkv[:], reads=[bpkv], writes=[bSSf[n], bSSb[n]])
                                fwd_step(n)
                                yield

                        load_x(s, 0, 0)
                        if NG > 1:
                            load_x(s, 1, 1)
                        fgs = {g: front_gen(g % 2, "p1", s, g) for g in range(NG)}
                        drive(fgs[0])
                        if NG > 1:
                            next(fgs[1])
                        for gi in range(NG):
                            slot = gi % 2
                            if gi + 2 < NG:
                                load_x(s, gi + 2, slot)
                            fa = fgs.get(gi + 1)
                            fb = fgs.get(gi + 2)
                            tg = p1_tiles(gi, slot)
                            for ch in "ABBBBACBBA":
                                g = {"A": fa, "B": tg, "C": fb}[ch]
                                if g is not None:
                                    next(g, None)
                            drive(tg, fa)

                        def p2_A(gi, slot, fg):
                            def tok(t):
                                tsl = slice(t * 128, (t + 1) * 128)
                                pw, bpw = psum.next()
                                for kc in range(KC):
                                    P.mm(pw[:, :], hb[slot][:, kc, tsl], wint[:, kc, 512:1024], start=(kc == 0),
                                         stop=(kc == KC - 1), reads=[bh[slot][kc // 6], bW], writes=[bpw])
                                P.act(gsvt[t][:], pw[:], AF.Gelu, reads=[bpw], writes=[bgsvt[t]])
                                P.tt("pool", sqv[:], gsvt[t][:], gsvt[t][:], ALU.mult, reads=[bgsvt[t]], writes=[bsqv])
                                P.op("dve", lambda e: e.tensor_reduce(
                                    out=ss4[:, 4 * t:4 * t + 4], in_=sqv[:].rearrange("p (g c) -> p g c", g=4),
                                    axis=AX.X, op=ALU.add), reads=[bsqv], pwrites=[bss4])

                            def vnorm():
                                P.act(ss4b[:], ss4[:], AF.Ln, reads=[bss4], writes=[bss4b], scale=1.0 / 128, bias=EPS)
                                P.act(ss4b[:], ss4b[:], AF.Exp, reads=[bss4b], writes=[bss4b], scale=-0.5)
                                for t in range(TPG):
                                    P.tt("dve", vn[slot][t][:], gsvt[t][:].rearrange("p (g c) -> p g c", g=4),
                                         ss4b[:, 4 * t:4 * t + 4].unsqueeze(2).broadcast_to([128, 4, 128]), ALU.mult,
                                         reads=[bgsvt[t], bss4b], writes=[bvn[slot][t]])

                            def decays():
                                for t in range(TPG):
                                    tsl = slice(t * 128, (t + 1) * 128)
                                    for bi in range(2):
                                        pb, bpb = psum.next()
                                        for j in range(2):
                                            hh = 2 * bi + j
                                            P.mm(pb[:, j * 256:(j + 1) * 256],
                                                 lsb[slot][t][:, hh * 128:(hh + 1) * 128], ucat_b[:, :],
                                                 reads=[blsb[slot][t], bconst], writes=[bpb])
                                        pbv = pb[:].rearrange("p (h c) -> p h c", h=2)
                                        P.act(Eb[0:64, 2 * bi:2 * bi + 2, tsl], pbv[0:64, :, 0:128], AF.Exp,
                                              reads=[bpb], pwrites=[bEb])
                                        P.act(Eb[64:128, 2 * bi:2 * bi + 2, tsl], pbv[64:128, :, 128:256], AF.Exp,
                                              reads=[bpb], pwrites=[bEb])
                                        P.act(Einv[0:64, 2 * bi:2 * bi + 2, tsl], pbv[0:64, :, 0:128], AF.Exp,
                                              reads=[bpb], pwrites=[bEinv], scale=-1.0)
                                        P.act(Einv[64:128, 2 * bi:2 * bi + 2, tsl], pbv[64:128, :, 128:256], AF.Exp,
                                              reads=[bpb], pwrites=[bEinv], scale=-1.0)

                            def feat(pr):
                                pf, bpf = psum.next()
                                for j in range(2):
                                    blk = 2 * pr + j
                                    for kc in range(KC):
                                        P.mm(pf[:, j * G:(j + 1) * G], winf[:, kc, blk * 128:(blk + 1) * 128],
                                             hb[slot][:, kc, :], start=(kc == 0), stop=(kc == KC - 1),
                                             reads=[bh[slot][kc // 6], bW], writes=[bpf])
                                pfv = pf[:].rearrange("p (b t) -> p b t", b=2)
                                b2 = slice(2 * (pr % 2), 2 * (pr % 2) + 2)
                                if pr < 2:
                                    P.tt("dve", qt[slot][:, b2, :], pfv, Eb[:, b2, :], ALU.mult, reads=[bpf, bEb],
                                         pwrites=[bqt[slot]])
                                elif pr < 4:
                                    P.tt("dve", kt[slot][:, b2, :], pfv, Einv[:, b2, :], ALU.mult,
                                         reads=[bpf, bEinv], pwrites=[bkt[slot]])
                                elif pr < 6:
                                    P.act(sg[slot][:, b2, :], pfv, AF.Silu, reads=[bpf], pwrites=[bsg[slot]])
                                else:
                                    P.act(gu[slot][:, b2, :], pfv, AF.Gelu, reads=[bpf], pwrites=[bgu[slot]])

                            next(fg)
                            yield
                            next(fg)
                            yield
                            next(fg)
                            yield
                            decays()
                            yield
                            tok(0)
                            yield
                            tok(1)
                            yield
                            for pr in (4, 5, 6, 7, 0, 1, 2, 3):
                                feat(pr)
                                yield
                            vnorm()
                            yield

                        def p2_B(gi, slot, g_next2=None):
                            po_l, pm_l = [], []
                            for t in range(TPG):
                                n = gi * TPG + t
                                tsl = slice(t * 128, (t + 1) * 128)
                                psf, bpsf = psum.next()
                                psb, bpsb = psum.next()
                                for hh in range(4):
                                    hs = slice(hh * 128, (hh + 1) * 128)
                                    P.mm(psf[:, hs], kt[slot][0:64, hh, tsl], qt[slot][0:64, hh, tsl],
                                         reads=[bkt[slot], bqt[slot]], writes=[bpsf])
                                    P.mm(psb[:, hs], kt[slot][64:128, hh, tsl], qt[slot][64:128, hh, tsl],
                                         reads=[bkt[slot], bqt[slot]], writes=[bpsb])
                                P.tt("dve", PT[t][:, :, 0, :], psf[:].rearrange("p (h c) -> p h c", h=4),
                                     cst[:, 768:896].unsqueeze(1).broadcast_to([128, 4, 128]), ALU.mult,
                                     reads=[bpsf, bcst], pwrites=[bPT[t]])
                                P.tt("dve", PT[t][:, :, 1, :], psb[:].rearrange("p (h c) -> p h c", h=4),
                                     cst[:, 896:1024].unsqueeze(1).broadcast_to([128, 4, 128]), ALU.mult,
                                     reads=[bpsb, bcst], pwrites=[bPT[t]])
                                pm, bpm = psum.next()
                                for g_ in range(4):
                                    gs = slice(g_ * 128, (g_ + 1) * 128)
                                    P.mm(pm[:, gs], vn[slot][t][:, g_, :], wst_b[:, g_, :], reads=[bvn[slot][t], bW],
                                         writes=[bpm])
                                for g_ in range(4):
                                    gs = slice(g_ * 128, (g_ + 1) * 128)
                                    P.stt("dve", sgt[:, g_, :], pm[:, gs],
                                          vecs[:, V_SGU + l * 4 + g_:V_SGU + l * 4 + g_ + 1], bs_bc[:, gs],
                                          ALU.mult, ALU.add, reads=[bpm, bconst, bW], pwrites=[bsgt])
                                P.tt("pool", mT[:, 4:8, tsl], sgt[:], gu[slot][:, :, tsl], ALU.mult,
                                     reads=[bsgt, bgu[slot]], pwrites=[bmT])
                                yield
                            for t in range(TPG):
                                n = gi * TPG + t
                                tsl = slice(t * 128, (t + 1) * 128)
                                po, bpo = psum.next(hold=True)
                                for hh in range(4):
                                    hs = slice(hh * 128, (hh + 1) * 128)
                                    P.mm(po[:, hs], vtok[slot][t][:, hs], PT[t][:, hh, 0, :], start=True, stop=False,
                                         reads=[bvtok[slot][t], bPT[t]], writes=[bpo])
                                    P.mm(po[:, hs], vtok[slot][t][:, hs], PT[t][:, hh, 1, :], start=False,
                                         stop=False, reads=[bvtok[slot][t], bPT[t]], writes=[bpo])
                                    P.mm(po[:, hs], SS[:, n, hs], qt[slot][:, hh, tsl], start=False, stop=True,
                                         reads=[bSSf[n], bSSb[n], bqt[slot]], writes=[bpo])
                                P.act(osq[:], po[:], AF.Square, reads=[bpo], writes=[bosq])
                                po_l.append((po, bpo))
                                yield
                                pn, bpn = psum.next()
                                P.mm(pn[:, :], ones_b[:], osq[:], reads=[bosq, bconst], writes=[bpn])
                                P.act(ro[:], pn[:], AF.Ln, reads=[bpn], writes=[bro], scale=1.0 / 128, bias=EPS)
                                P.act(ro[:], ro[:], AF.Exp, reads=[bro], writes=[bro], scale=-0.5)
                                P.tt("dve", om[:], po[:], ro[:], ALU.mult, reads=[bpo, bro], writes=[bom])
                                psum.release(bpo)
                                P.tt("pool", mT[:, 0:4, tsl], om[:].rearrange("p (h c) -> p h c", h=4),
                                     sg[slot][:, :, tsl], ALU.mult, reads=[bom, bsg[slot]], pwrites=[bmT])
                                yield
                            for pr in range(4):
                                pq, bpq = psum.next()
                                for j in range(2):
                                    dmc = 2 * pr + j
                                    for mc in range(KC):
                                        P.mm(pq[:, j * G:(j + 1) * G], wout[:, mc, dmc * 128:(dmc + 1) * 128],
                                             mT[:, mc, :], start=(mc == 0), stop=(mc == KC - 1),
                                             reads=[bmT, bW], writes=[bpq])
                                hf = pr // 2
                                P.tt("dve", xo[:, 2 * (pr % 2):2 * (pr % 2) + 2, :],
                                     pq[:].rearrange("p (b t) -> p b t", b=2), xg[slot][:, 2 * pr:2 * pr + 2, :],
                                     ALU.add, reads=[bpq, bxg[slot][pr]], pwrites=[bxo])
                                if pr % 2 == 1:
                                    t0_ = s * seq + gi * G
                                    P.dma(xs[hf * 512:(hf + 1) * 512, t0_:t0_ + G].rearrange("(kc p) t -> p kc t", p=128),
                                          xo[:], f"stx{hf}", reads=[bxo], writes=[bxs_h[hf]])
                                    if hf == 1:
                                        bxs[s][gi].w = bxs_h[1].w
                                        bxs[s][gi].r = {}
                                        bxs[s][gi].pw = dict([bxs_h[0].w])
                                yield
                            if g_next2 is not None:
                                load_x(s, g_next2, slot)

                        order = list(range(NG - 1, -1, -1))
                        load_x(s, order[0], 0)
                        if NG > 1:
                            load_x(s, order[1], 1)
                        fgs = {oi: front_gen(oi % 2, "p2", s, order[oi]) for oi in range(NG)}
                        for n in (order[0] * TPG + 1, order[0] * TPG):
                            bwd_step(n)
                        next(fgs[0])
                        drive(p2_A(order[0], 0, fgs[0]))
                        if NG > 1:
                            next(fgs[1])
                        for oi, gi in enumerate(order):
                            slot = oi % 2
                            nxt_g = order[oi + 1] if oi + 1 < NG else None
                            if nxt_g is not None:
                                for n in (nxt_g * TPG + 1, nxt_g * TPG):
                                    bwd_step(n)
                            ga = p2_A(nxt_g, 1 - slot, fgs[oi + 1]) if nxt_g is not None else None
                            gb = p2_B(gi, slot, order[oi + 2] if oi + 2 < NG else None)
                            drive_pattern("BBBABBABAAAAAABBBBAAAAAAA", ga, gb)
                            if oi + 2 < NG:
                                next(fgs[oi + 2])
                    P.barrier()
                    P.flush(sems)
                    if stop == 4:
                        return nc

            with contextlib.ExitStack() as ph:
                EP = ph.enter_context
                w1b = EP(sbt("w1b", [128, KC, DFF], BF16))
                w2b = EP(sbt("w2b", [128, FC, D], BF16))
                bW = Buf()
                with contextlib.ExitStack() as ld:
                    EL = ld.enter_context
                    NST = 3
                    stg = [EL(sbt(f"stgm{i}", [128, DFF], F32)) for i in range(NST)]
                    bstg = [Buf() for _ in range(NST)]
                    ceng = ("dve", "act", "dve", "act", "dve", "act", "pool", "dve")
                    n_ld = 0
                    for kc in range(KC):
                        sl = n_ld % NST
                        n_ld += 1
                        P.dma(stg[sl][:], w1[l * D + kc * 128:l * D + (kc + 1) * 128, :], f"ldw{sl}",
                              writes=[bstg[sl]])
                        gm = vecs[:, V_MLP + l * 8 + kc:V_MLP + l * 8 + kc + 1]
                        for qd in range(8):
                            eng = ceng[qd]
                            o_ = w1b[:, kc, qd * 512:(qd + 1) * 512]
                            i_ = stg[sl][:, qd * 512:(qd + 1) * 512]
                            if eng == "act":
                                P.act(o_, i_, AF.Copy, reads=[bstg[sl], bconst], writes=[bW], scale=gm)
                            else:
                                P.ts(eng, o_, i_, gm, ALU.mult, reads=[bstg[sl], bconst], pwrites=[bW])
                    for j in range(FC // 4):
                        sl = n_ld % NST
                        n_ld += 1
                        P.dma(stg[sl][:].rearrange("p (a n) -> p a n", a=4),
                              w2[l * DFF + j * 512:l * DFF + (j + 1) * 512, :].rearrange("(a p) n -> p a n", p=128),
                              f"ldw{sl}", writes=[bstg[sl]])
                        for qd in range(8):
                            eng = ceng[qd]
                            P.cp(eng, w2b[:, 4 * j + qd // 2, (qd % 2) * 512:(qd % 2 + 1) * 512],
                                 stg[sl][:, qd * 512:(qd + 1) * 512], reads=[bstg[sl]], pwrites=[bW])
                    P.barrier()
                    P.flush(sems)
                with contextlib.ExitStack() as cs:
                    EC = cs.enter_context
                    xg = [EC(sbt(f"xm{i}", [128, KC, G], F32)) for i in range(2)]
                    bxg = [[Buf() for _ in range(4)] for _ in range(2)]
                    h2 = [EC(sbt(f"h2_{i}", [128, KC, G], BF16)) for i in range(2)]
                    bh2 = [Buf(), Buf()]
                    sq = EC(sbt("sqm", [128, KC, G], BF16)); bsq = Buf()
                    sqs = EC(sbt("sqsm", [128, G], BF16)); bsqs = Buf()
                    lnt = EC(sbt("lntm", [128, G], F32)); blnt = Buf()
                    rstd = EC(sbt("rstdm", [128, G], F32)); brstd = Buf()
                    h1 = EC(sbt("h1", [128, FC, G], BF16))
                    bh1 = [Buf() for _ in range(FC // 2)]
                    rl = [EC(sbt(f"rl{i}", [128, 2 * G], F32)) for i in range(3)]
                    brl = [Buf() for _ in range(3)]

                    def _red(e):
                        with nc.allow_low_precision("8-term sum of squares feeding a bf16 matmul operand"):
                            return e.tensor_reduce(out=sqs[:], in_=sq[:].rearrange("p kc t -> p t kc"),
                                                   axis=AX.X, op=ALU.add)

                    def norm_early(slot):
                        P.act(sq[:], xg[slot][:], AF.Square, reads=bxg[slot], writes=[bsq])
                        if not last:
                            P.op("dve", _red, reads=[bsq], writes=[bsqs])

                    def norm_late(slot):
                        ps, bps = psum.next()
                        if last:
                            for kc in range(KC):
                                P.mm(ps[:, 0:G], ones_b[:], sq[:, kc, :], start=(kc == 0), stop=(kc == KC - 1),
                                     reads=[bsq, bconst], writes=[bps])
                        else:
                            P.mm(ps[:, 0:G], ones_b[:], sqs[:], reads=[bsqs, bconst], writes=[bps])
                        P.act(lnt[:], ps[:, 0:G], AF.Ln, reads=[bps], writes=[blnt], scale=1.0 / D, bias=EPS)
                        P.act(rstd[:], lnt[:], AF.Exp, reads=[blnt], writes=[brstd], scale=-0.5)

                    def norm_stats(slot):
                        norm_early(slot)
                        norm_late(slot)

                    def h2_mul(slot):
                        P.tt("dve", h2[slot][:], xg[slot][:], rstd[:].unsqueeze(1).broadcast_to([128, KC, G]),
                             ALU.mult, reads=bxg[slot] + [brstd], writes=[bh2[slot]])

                    def front_m(slot):
                        norm_stats(slot)
                        h2_mul(slot)

                    seq_groups = [(s, gi) for s in range(nseq) for gi in range(NG)]

                    def load_m(idx, slot):
                        s, gi = seq_groups[idx]
                        P.dma(xg[slot][:], xview(xs, s, gi), f"ldx{slot}", reads=[bxs[s][gi]], writes=bxg[slot])

                    def finish_final(pslot, ps_, pgi):
                        norm_late(pslot)
                        for kc in range(KC):
                            P.stt("dve", xg[pslot][:, kc, :], xg[pslot][:, kc, :],
                                  vecs[:, V_FIN + kc:V_FIN + kc + 1], rstd[:], ALU.mult, ALU.mult,
                                  reads=[bxg[pslot][kc // 2], brstd, bconst], writes=[bxg[pslot][kc // 2]])
                        P.dma(xview(yT, ps_, pgi), xg[pslot][:], f"stx{pslot}", reads=bxg[pslot],
                              writes=[bxs[ps_][pgi]])

                    pending = None
                    load_m(0, 0)
                    front_m(0)
                    nrl = 0
                    for idx, (s, gi) in enumerate(seq_groups):
                        slot = idx % 2
                        if not last and idx + 1 < len(seq_groups):
                            load_m(idx + 1, 1 - slot)
                        for pr in range(FC // 2):
                            pf, bpf = psum.next()
                            for j in range(2):
                                fc = 2 * pr + j
                                for kc in range(KC):
                                    P.mm(pf[:, j * G:(j + 1) * G], w1b[:, kc, fc * 128:(fc + 1) * 128],
                                         h2[slot][:, kc, :], start=(kc == 0), stop=(kc == KC - 1),
                                         reads=[bh2[slot], bW], writes=[bpf])
                            r_ = nrl % 3
                            nrl += 1
                            if pr % 2 == 0:
                                P.act(rl[r_][:], pf[:], AF.Relu, reads=[bpf], writes=[brl[r_]])
                            else:
                                P.ts("dve", rl[r_][:], pf[:], 0.0, ALU.max, reads=[bpf], writes=[brl[r_]])
                            P.tt("pool", h1[:, 2 * pr:2 * pr + 2, :], rl[r_][:].rearrange("p (b t) -> p b t", b=2),
                                 rl[r_][:].rearrange("p (b t) -> p b t", b=2), ALU.mult, reads=[brl[r_]],
                                 writes=[bh1[pr]])
                            if last and pr == 5:
                                if pending is not None:
                                    finish_final(*pending)
                                    pending = None
                                if idx + 1 < len(seq_groups):
                                    load_m(idx + 1, 1 - slot)
                            if idx + 1 < len(seq_groups):
                                if pr == (12 if last else 7):
                                    norm_early(1 - slot)
                                elif pr == (15 if last else 12):
                                    norm_late(1 - slot)
                                    h2_mul(1 - slot)
                        for pr in range(4):
                            pq, bpq = psum.next()
                            for j in range(2):
                                dmc = 2 * pr + j
                                for fc in range(FC):
                                    P.mm(pq[:, j * G:(j + 1) * G], w2b[:, fc, dmc * 128:(dmc + 1) * 128], h1[:, fc, :],
                                         start=(fc == 0), stop=(fc == FC - 1), reads=[bh1[fc // 2], bW],
                                         writes=[bpq])
                            P.tt("dve", xg[slot][:, 2 * pr:2 * pr + 2, :], pq[:].rearrange("p (b t) -> p b t", b=2),
                                 xg[slot][:, 2 * pr:2 * pr + 2, :], ALU.add, reads=[bpq, bxg[slot][pr]],
                                 writes=[bxg[slot][pr]])
                        if not last:
                            P.dma(xview(xs, s, gi), xg[slot][:], f"stx{slot}", reads=bxg[slot],
                                  writes=[bxs[s][gi]])
                        else:
                            norm_early(slot)
                            pending = (slot, s, gi)
                    if pending is not None:
                        finish_final(*pending)
                    P.barrier()
                    P.flush(sems)
    return nc


def _consts():
    j = np.arange(128)[:, None]
    c = np.arange(128)[None, :]
    s = np.float32(-1.0 / 16.0)
    cst = np.zeros((128, 1025), np.float32)
    cst[:, 0:128] = (j > c) * s
    cst[:, 128:256] = (j < c) * s
    cst[:, 256:384] = (j <= c) * s
    cst[:, 384:512] = (j >= c) * s
    cst[:, 512:640] = 1.0
    cst[:, 768:896] = (c >= j)
    cst[:, 896:1024] = (j > c)
    cst[:, 1024] = s
    return cst


def make_shared(depth, norm_mix_g, w_in, w_a2_fwd, b_a_fwd, w_a2_bwd, b_a_bwd, gla_norm_g, sgu_norm_g, w_s, b_s,
                w_out, norm_mlp_g, w_mlp1, w_mlp2, final_norm_g):
    f = lambda a: np.ascontiguousarray(np.asarray(a, dtype=np.float32))
    L = depth
    vecs = np.concatenate([
        f(norm_mix_g).reshape(L, 8, 128).transpose(2, 0, 1).reshape(128, 8 * L),
        f(norm_mlp_g).reshape(L, 8, 128).transpose(2, 0, 1).reshape(128, 8 * L),
        f(gla_norm_g).reshape(L, 4, 128).transpose(2, 0, 1).reshape(128, 4 * L),
        f(sgu_norm_g).reshape(L, 4, 128).transpose(2, 0, 1).reshape(128, 4 * L),
        f(final_norm_g).reshape(8, 128).T,
    ], axis=1)
    return {
        "w_in": f(w_in).reshape(L * D, DIN),
        "w_out": f(w_out).reshape(L * D, D),
        "w1": f(w_mlp1).reshape(L * D, DFF),
        "w2": f(w_mlp2).reshape(L * DFF, D),
        "wa2f": f(w_a2_fwd).reshape(L * 16, 256),
        "wa2b": f(w_a2_bwd).reshape(L * 16, 256),
        "baf": f(b_a_fwd).reshape(L, 256),
        "bab": f(b_a_bwd).reshape(L, 256),
        "wsT": f(np.asarray(w_s, np.float32).transpose(0, 1, 3, 2)).reshape(L * 4 * 128, 128),
        "bs": f(b_s).reshape(L, 512),
        "vecs": f(vecs),
        "cst": _consts(),
    }


def run(x, shared, depth, n_cores, nseq, stop=99):
    x = np.asarray(x, dtype=np.float32)
    B, S, _ = x.shape
    assert B == n_cores * nseq
    nc = build_program(depth, nseq, S, stop)
    in_maps = []
    for c in range(n_cores):
        xT = np.ascontiguousarray(x[c * nseq:(c + 1) * nseq].reshape(nseq * S, D).T)
        m = dict(shared)
        m["xT"] = xT
        in_maps.append(m)
    res = run_bass_kernel_spmd(nc, in_maps, core_ids=list(range(n_cores)))
    out = np.empty((B, S, D), np.float32)
    for c in range(n_cores):
        out[c * nseq:(c + 1) * nseq] = res.results[c]["yT"].T.reshape(nseq, S, D)
    return out


def kernel(x, norm_mix_g, w_in, w_a2_fwd, b_a_fwd, w_a2_bwd, b_a_bwd, gla_norm_g, sgu_norm_g, w_s, b_s, w_out,
           norm_mlp_g, w_mlp1, w_mlp2, final_norm_g):
    depth = 4
    shared = make_shared(depth, norm_mix_g, w_in, w_a2_fwd, b_a_fwd, w_a2_bwd, b_a_bwd, gla_norm_g, sgu_norm_g,
                         w_s, b_s, w_out, norm_mlp_g, w_mlp1, w_mlp2, final_norm_g)
    return run(x, shared, depth, 8, 2)
```
